# Optimizing a Trainium2 kernel written in Bass

```python
import math
import jax, jax.numpy as jnp
from jax import lax
import numpy as np

D_MODEL = 1024
BATCH = 8
SEQ = 4096
DEPTH = 1

RET_HEADS = 8
RET_DK = 128
RET_DV = 256
RET_CHUNK = 128
ROPE_BASE = 10000.0
CONV_CH = D_MODEL
CONV_WIDTH = 31
D_FF = 2816
LN_EPS = 1e-5

QK_W = RET_HEADS * RET_DK
V_W = RET_HEADS * RET_DV
SPLIT_POINTS = tuple(int(p) for p in np.cumsum([QK_W, QK_W, V_W, V_W, CONV_CH, CONV_CH, D_MODEL]))
IN_W = 2 * QK_W + 2 * V_W + 2 * CONV_CH + 2 * D_MODEL

kernel_name = "hybrid_retention_conformer_macaron_deepnorm"


def layer_norm(x, g, b):
    xf = x.astype(jnp.float32)
    mu = jnp.mean(xf, axis=-1, keepdims=True)
    var = jnp.mean(jnp.square(xf - mu), axis=-1, keepdims=True)
    y = (xf - mu) * lax.rsqrt(var + LN_EPS) * g.astype(jnp.float32) + b.astype(jnp.float32)
    return y.astype(x.dtype)


def swiglu_ffn(h, w_gate, w_up, w_down):
    return (jax.nn.silu(h @ w_gate) * (h @ w_up)) @ w_down


def rotary(x, cos, sin):
    half = x.shape[-1] // 2
    x1, x2 = x[..., :half], x[..., half:]
    return jnp.concatenate([x1 * cos - x2 * sin, x2 * cos + x1 * sin], axis=-1).astype(x.dtype)


def retention_chunkwise(q, k, v):
    b, s, h, dk = q.shape
    dv = v.shape[-1]
    c = RET_CHUNK
    n = s // c
    log_g = jnp.log(1.0 - jnp.exp2(-5.0 - jnp.arange(h, dtype=jnp.float32)))
    idx = jnp.arange(c, dtype=jnp.float32)
    diff = idx[:, None] - idx[None, :]
    decay_mask = jnp.where(diff[None] >= 0,
                           jnp.exp(jnp.maximum(diff, 0.0)[None] * log_g[:, None, None]), 0.0)
    qc = q.reshape(b, n, c, h, dk)
    kc = k.reshape(b, n, c, h, dk)
    vc = v.reshape(b, n, c, h, dv)
    scores = jnp.einsum('bnihd,bnjhd->bnhij', qc, kc) * decay_mask
    intra = jnp.einsum('bnhij,bnjhe->bnihe', scores, vc)
    xi = jnp.exp((idx[:, None] + 1.0) * log_g[None, :])
    zeta = jnp.exp((c - 1.0 - idx)[:, None] * log_g[None, :])
    chunk_decay = jnp.exp(c * log_g)

    def step(state, xs):
        qn, kn, vn = xs
        cross = jnp.einsum('bihd,bhde->bihe', qn, state) * xi[None, :, :, None]
        new_state = chunk_decay[None, :, None, None] * state + jnp.einsum(
            'bjhd,bjhe->bhde', kn * zeta[None, :, :, None], vn)
        return new_state, cross

    state0 = jnp.zeros((b, h, dk, dv), jnp.float32)
    _, cross = lax.scan(step, state0, (jnp.moveaxis(qc, 1, 0), jnp.moveaxis(kc, 1, 0), jnp.moveaxis(vc, 1, 0)))
    cross = jnp.moveaxis(cross, 0, 1)
    return (intra + cross).reshape(b, s, h, dv)


def hybrid_mixer(h, w_in, b_in, ret_gn_g, conv_k, conv_b, conv_ln_g, conv_ln_b,
                 w_ret_o, w_conv_o, w_out, cos, sin):
    bsz, s, _ = h.shape
    proj = h @ w_in + b_in
    q, k, v, g, glu_a, glu_b, gate_r, gate_c = jnp.split(proj, SPLIT_POINTS, axis=-1)

    q = rotary(q.reshape(bsz, s, RET_HEADS, RET_DK), cos, sin) * (RET_DK ** -0.5)
    k = rotary(k.reshape(bsz, s, RET_HEADS, RET_DK), cos, sin)
    v = v.reshape(bsz, s, RET_HEADS, RET_DV)
    r = retention_chunkwise(q, k, v)
    mu = jnp.mean(r, axis=-1, keepdims=True)
    var = jnp.mean(jnp.square(r - mu), axis=-1, keepdims=True)
    r = ((r - mu) * lax.rsqrt(var + LN_EPS)).reshape(bsz, s, V_W) * ret_gn_g.astype(jnp.float32)
    ret_out = (jax.nn.silu(g) * r.astype(h.dtype)) @ w_ret_o

    u = glu_a * jax.nn.sigmoid(glu_b)
    u = lax.conv_general_dilated(u, conv_k, window_strides=(1,), padding=[(CONV_WIDTH - 1, 0)],
                                 dimension_numbers=('NWC', 'WIO', 'NWC'),
                                 feature_group_count=CONV_CH) + conv_b
    u = jax.nn.silu(layer_norm(u, conv_ln_g, conv_ln_b))
    conv_out = u @ w_conv_o

    merged = jax.nn.sigmoid(gate_r) * ret_out + jax.nn.sigmoid(gate_c) * conv_out
    return merged @ w_out


def setup_inputs(seed: int = 0) -> dict:
    key = jax.random.key(seed)
    ks = jax.random.split(key, 24)
    beta = (8.0 * DEPTH) ** -0.25
    L = DEPTH

    def nrm(k, shape, scale):
        return jax.random.normal(k, shape, jnp.float32) * scale

    def gain(k, shape):
        return 1.0 + 0.02 * jax.random.normal(k, shape, jnp.float32)

    return {
        "x": jax.random.normal(ks[0], (BATCH, SEQ, D_MODEL), jnp.float32),
        "ffn1_w_gate": nrm(ks[1], (L, D_MODEL, D_FF), D_MODEL ** -0.5),
        "ffn1_w_up": nrm(ks[2], (L, D_MODEL, D_FF), D_MODEL ** -0.5),
        "ffn1_w_down": nrm(ks[3], (L, D_FF, D_MODEL), beta * D_FF ** -0.5),
        "ln1_g": gain(ks[4], (L, D_MODEL)),
        "ln1_b": nrm(ks[5], (L, D_MODEL), 0.02),
        "w_in": nrm(ks[6], (L, D_MODEL, IN_W), D_MODEL ** -0.5),
        "b_in": nrm(ks[7], (L, IN_W), 0.02),
        "ret_gn_g": gain(ks[8], (L, V_W)),
        "conv_k": nrm(ks[9], (L, CONV_WIDTH, 1, CONV_CH), CONV_WIDTH ** -0.5),
        "conv_b": nrm(ks[10], (L, CONV_CH), 0.02),
        "conv_ln_g": gain(ks[11], (L, CONV_CH)),
        "conv_ln_b": nrm(ks[12], (L, CONV_CH), 0.02),
        "w_ret_o": nrm(ks[13], (L, V_W, D_MODEL), beta * V_W ** -0.5),
        "w_conv_o": nrm(ks[14], (L, CONV_CH, D_MODEL), beta * CONV_CH ** -0.5),
        "w_out": nrm(ks[15], (L, D_MODEL, D_MODEL), beta * D_MODEL ** -0.5),
        "ln2_g": gain(ks[16], (L, D_MODEL)),
        "ln2_b": nrm(ks[17], (L, D_MODEL), 0.02),
        "ffn2_w_gate": nrm(ks[18], (L, D_MODEL, D_FF), D_MODEL ** -0.5),
        "ffn2_w_up": nrm(ks[19], (L, D_MODEL, D_FF), D_MODEL ** -0.5),
        "ffn2_w_down": nrm(ks[20], (L, D_FF, D_MODEL), beta * D_FF ** -0.5),
        "ln3_g": gain(ks[21], (L, D_MODEL)),
        "ln3_b": nrm(ks[22], (L, D_MODEL), 0.02),
    }


def reference(x, ffn1_w_gate, ffn1_w_up, ffn1_w_down, ln1_g, ln1_b, w_in, b_in, ret_gn_g,
              conv_k, conv_b, conv_ln_g, conv_ln_b, w_ret_o, w_conv_o, w_out, ln2_g, ln2_b,
              ffn2_w_gate, ffn2_w_up, ffn2_w_down, ln3_g, ln3_b):
    alpha = (2.0 * DEPTH) ** 0.25
    s = x.shape[1]
    half = RET_DK // 2
    freqs = ROPE_BASE ** (-jnp.arange(half, dtype=jnp.float32) / half)
    ang = jnp.arange(s, dtype=jnp.float32)[:, None] * freqs[None, :]
    cos = jnp.cos(ang)[:, None, :]
    sin = jnp.sin(ang)[:, None, :]

    for l in range(DEPTH):
        x = layer_norm(alpha * x + 0.5 * swiglu_ffn(x, ffn1_w_gate[l], ffn1_w_up[l], ffn1_w_down[l]),
                       ln1_g[l], ln1_b[l])
        m = hybrid_mixer(x, w_in[l], b_in[l], ret_gn_g[l], conv_k[l], conv_b[l], conv_ln_g[l],
                         conv_ln_b[l], w_ret_o[l], w_conv_o[l], w_out[l], cos, sin)
        x = layer_norm(alpha * x + m, ln2_g[l], ln2_b[l])
        x = layer_norm(alpha * x + 0.5 * swiglu_ffn(x, ffn2_w_gate[l], ffn2_w_up[l], ffn2_w_down[l]),
                       ln3_g[l], ln3_b[l])
    return x
```

```python
from contextlib import ExitStack
import numpy as np
import concourse.bass as bass
import concourse.mybir as mybir
from concourse.bass_utils import run_bass_kernel_spmd

F32 = mybir.dt.float32
BF16 = mybir.dt.bfloat16
AF = mybir.ActivationFunctionType
ALU = mybir.AluOpType

P = 128
D = 1024
SEQ = 4096
TT = 512
NT = SEQ // TT
DFF = 2816
NFC = DFF // P
H = 8
DV = 256
CW = 31
HALO = CW - 1
ALPHA = float(2.0 ** 0.25)
EPS = 1e-5
NS = 4
SLOT = 4096
NB = 6
NPE = 23


class Op:
    __slots__ = ("eng", "fn", "deps", "signaled", "tok", "idx", "dma_key", "is_dma", "gidx", "dma_val", "final")


class Sched:
    ENGS = ("pe", "act", "dve", "pool", "sp")
    WINDOW = 10 ** 9

    def __init__(self):
        self.ops = {e: [] for e in self.ENGS}
        self.last_writer = {}
        self.readers = {}
        self.dma_count = {}
        self.n = 0

    def add(self, eng, fn, reads=(), writes=(), dma=None, final=False):
        o = Op()
        o.eng = eng
        o.fn = fn
        o.signaled = False
        o.is_dma = dma is not None
        o.dma_key = dma
        o.gidx = self.n
        o.tok = 0
        o.dma_val = 0
        o.final = final
        self.n += 1
        deps = {}
        for r in reads:
            w = self.last_writer.get(r)
            if w is not None:
                deps[w.gidx] = w
        for k in writes:
            w = self.last_writer.get(k)
            if w is not None:
                deps[w.gidx] = w
            rd = self.readers.get(k)
            if rd:
                for x in rd.values():
                    deps[x.gidx] = x
        o.deps = list(deps.values())
        for d in o.deps:
            if not d.is_dma:
                d.signaled = True
        rk = ("dma", o.gidx) if o.is_dma else eng
        for r in reads:
            self.readers.setdefault(r, {})[rk] = o
        for k in writes:
            self.last_writer[k] = o
            self.readers[k] = {}
        if o.is_dma:
            c = self.dma_count.get(dma, 0) + 1
            self.dma_count[dma] = c
            o.dma_val = 16 * c
        o.idx = len(self.ops[eng])
        self.ops[eng].append(o)
        return o

    def finalize(self):
        for e in self.ENGS:
            c = 0
            for o in self.ops[e]:
                if o.is_dma:
                    if o.final:
                        o.dma_val = 16 * self.dma_count[o.dma_key]
                elif o.signaled:
                    c += 1
                    o.tok = c

    def sem_keys(self):
        return list(self.ENGS) + [("dma", k) for k in self.dma_count]

    def run_engine(self, e, engobj, sems):
        known = {}
        for o in self.ops[e]:
            for d in o.deps:
                if d.is_dma:
                    sk = ("dma", d.dma_key)
                    val = d.dma_val
                else:
                    if d.eng == e and (e == "pe" or (o.idx - d.idx) > self.WINDOW):
                        continue
                    sk = d.eng
                    val = d.tok
                if known.get(sk, 0) >= val:
                    continue
                engobj.wait_ge(sems[sk], val)
                known[sk] = val
            inst = o.fn(engobj)
            if o.is_dma:
                inst.then_inc(sems[("dma", o.dma_key)], 16)
            elif o.signaled:
                inst.then_inc(sems[e], 1)


CV = {}
_c = 0
for _n, _w in (("ln1_g", 8), ("ln1_b", 8), ("ln2_g", 8), ("ln2_b", 8), ("ln3_g", 8), ("ln3_b", 8),
               ("bq", 8), ("bqp", 8), ("bk", 8), ("bkp", 8), ("bg", 16), ("ba", 8),
               ("bb", 8), ("bgr", 8), ("bgc", 8),
               ("gn_g", 16), ("conv_b", 8), ("cln_g", 8), ("cln_b", 8), ("conv_k", 8 * CW), ("zeta", 8),
               ("A", 32), ("Y", 24), ("Z", 8 * CW), ("eps", 1)):
    CV[_n] = _c
    _c += _w
NCV = _c
NCV_IN = CV["A"]
CV["A_g1"] = CV["A"]
CV["A_b1"] = CV["A"] + 8
CV["A_g2"] = CV["A"] + 16
CV["A_b2"] = CV["A"] + 24


def _blk(W, c0, width=128):
    K = W.shape[0]
    return np.ascontiguousarray(W[:, c0:c0 + width].reshape(K // P, P, width).transpose(1, 0, 2)).reshape(P, -1)


def _piece(parts):
    out = np.zeros((P, SLOT), np.float32)
    o = 0
    for a in parts:
        out[:, o:o + a.shape[1]] = a
        o += a.shape[1]
    return out


def _ffn_pieces(wg, wu, wd):
    ps = []
    for i in range(NFC // 2):
        ps.append(_piece([_blk(wg, (2 * i) * P), _blk(wu, (2 * i) * P), _blk(wg, (2 * i + 1) * P), _blk(wu, (2 * i + 1) * P)]))
    for oc in range(8):
        ps.append(_piece([_blk(wd, oc * P)]))
    return ps


PIECES_PER_TILE = 19 + 4 + 16 + 2 + 8 + 2 + 19
PI_FFN1, PI_GLU, PI_HEAD, PI_CONVO, PI_MERGE, PI_OUT, PI_FFN2 = 0, 19, 23, 39, 41, 49, 51


def _build_wall(inp):
    w_in = inp["w_in"][0]
    perm = np.arange(1024).reshape(H, 2, 64)[:, ::-1, :].reshape(-1)
    Wq, Wk = w_in[:, 0:1024], w_in[:, 1024:2048]
    Wv, Wg = w_in[:, 2048:4096], w_in[:, 4096:6144]
    Wa, Wb = w_in[:, 6144:7168], w_in[:, 7168:8192]
    Wgr, Wgc = w_in[:, 8192:9216], w_in[:, 9216:10240]
    ps = []
    ps += _ffn_pieces(inp["ffn1_w_gate"][0], inp["ffn1_w_up"][0], inp["ffn1_w_down"][0])
    for i in range(4):
        ps.append(_piece([_blk(Wa, (2 * i) * P), _blk(Wb, (2 * i) * P), _blk(Wa, (2 * i + 1) * P), _blk(Wb, (2 * i + 1) * P)]))
    for h in range(H):
        ps.append(_piece([_blk(Wq, h * P), _blk(Wk, h * P), _blk(Wg, (2 * h) * P), _blk(Wg, (2 * h + 1) * P)]))
        ps.append(_piece([_blk(Wv, h * DV, DV)]))
    wco = inp["w_conv_o"][0]
    for i in range(2):
        ps.append(_piece([_blk(wco, (4 * i + j) * P) for j in range(4)]))
    wro = inp["w_ret_o"][0]
    for oc in range(8):
        ps.append(_piece([_blk(Wgr, oc * P), _blk(Wgc, oc * P), _blk(wro, oc * P)]))
    wo = inp["w_out"][0]
    for i in range(2):
        ps.append(_piece([_blk(wo, (4 * i + j) * P) for j in range(4)]))
    ps += _ffn_pieces(inp["ffn2_w_gate"][0], inp["ffn2_w_up"][0], inp["ffn2_w_down"][0])
    assert len(ps) == PIECES_PER_TILE
    return np.stack(ps, 0)


def _cols(v):
    return np.ascontiguousarray(np.asarray(v, np.float32).reshape(-1, P).T)


def _build_consts(inp):
    cv = np.zeros((P, NCV), np.float32)

    def put(name, v):
        c = _cols(v)
        cv[:, CV[name]:CV[name] + c.shape[1]] = c

    for n in ("ln1_g", "ln1_b", "ln2_g", "ln2_b", "ln3_g", "ln3_b"):
        put(n, inp[n][0])
    b_in = np.asarray(inp["b_in"][0], np.float32)
    perm = np.arange(1024).reshape(H, 2, 64)[:, ::-1, :].reshape(-1)
    put("bq", b_in[0:1024]); put("bqp", b_in[0:1024][perm])
    put("bk", b_in[1024:2048]); put("bkp", b_in[1024:2048][perm])
    put("bg", b_in[4096:6144]); put("ba", b_in[6144:7168]); put("bb", b_in[7168:8192])
    put("bgr", b_in[8192:9216]); put("bgc", b_in[9216:10240])
    put("gn_g", inp["ret_gn_g"][0]); put("conv_b", inp["conv_b"][0])
    put("cln_g", inp["conv_ln_g"][0]); put("cln_b", inp["conv_ln_b"][0])
    ck = np.asarray(inp["conv_k"][0], np.float32).reshape(CW, 8, P)
    cv[:, CV["conv_k"]:CV["conv_k"] + 8 * CW] = ck.transpose(2, 1, 0).reshape(P, 8 * CW)
    bv = np.ascontiguousarray(b_in[2048:4096].reshape(1, 2048))
    half = 64
    freqs = (np.float32(10000.0) ** (-np.arange(half, dtype=np.float32) / np.float32(half))).astype(np.float32)
    ang = (np.arange(SEQ, dtype=np.float32)[:, None] * freqs[None, :]).astype(np.float32)
    cos = np.cos(ang).astype(np.float32).T
    sin = np.sin(ang).astype(np.float32).T
    ctab = np.concatenate([cos, cos], 0)
    stab = np.concatenate([-sin, sin], 0)
    cs = np.stack([ctab, stab], 1).reshape(P, 2, NT, TT).transpose(2, 0, 1, 3)
    hh = np.arange(H, dtype=np.float32)
    log_g = np.log(1.0 - np.exp2(-5.0 - hh)).astype(np.float32)
    idx = np.arange(P, dtype=np.float32)
    diff = idx[None, :] - idx[:, None]
    sc = np.float32(P ** -0.5)
    dm = np.where(diff[:, None, :] >= 0, np.exp(np.maximum(diff, 0.0)[:, None, :] * log_g[None, :, None]), 0.0) * sc
    xi = np.exp((idx[None, :] + 1.0) * log_g[:, None]) * sc
    xit = np.broadcast_to(xi[None], (P, H, P))
    zeta = np.exp((P - 1.0 - idx)[:, None] * log_g[None, :])
    cv[:, CV["zeta"]:CV["zeta"] + 8] = zeta
    gC = [float(np.exp(np.float32(P) * log_g[h])) for h in range(H)]
    return (cv, bv, np.ascontiguousarray(cs, np.float32), np.ascontiguousarray(dm, np.float32),
            np.ascontiguousarray(xit, np.float32), gC)


def _gc_consts():
    hh = np.arange(H, dtype=np.float32)
    log_g = np.log(1.0 - np.exp2(-5.0 - hh)).astype(np.float32)
    return [float(np.exp(np.float32(P) * log_g[h])) for h in range(H)]


def build_program(ntiles=NT, dbg=None, stop=None):
    gC = _gc_consts()
    nc = bass.Bass("TRN2", target_bir_lowering=False)
    xT = nc.dram_tensor("xT", [NT, P, 8, TT], F32, kind="ExternalInput").ap()
    wall = nc.dram_tensor("wall", [PIECES_PER_TILE, P, SLOT], F32, kind="ExternalInput").ap()
    cvec = nc.dram_tensor("cvec", [P, NCV], F32, kind="ExternalInput").ap()
    bvrow = nc.dram_tensor("bvrow", [1, 2048], F32, kind="ExternalInput").ap()
    csd = nc.dram_tensor("cs", [NT, P, 2, TT], F32, kind="ExternalInput").ap()
    dmd = nc.dram_tensor("dmask", [P, H, P], F32, kind="ExternalInput").ap()
    xid = nc.dram_tensor("xit", [P, H, P], F32, kind="ExternalInput").ap()
    outT = nc.dram_tensor("outT", [NT, P, 8, TT], F32, kind="ExternalOutput").ap()
    wbf = nc.dram_tensor("wbf", [PIECES_PER_TILE, P, SLOT], BF16).ap()
    dgs = nc.dram_tensor("dgs", [8, P, NPE * P], BF16).ap()
    dbg_out = None
    if dbg is not None:
        dbg_out = nc.dram_tensor("dbg", [P, 8, TT], F32, kind="ExternalOutput").ap()
        dbg16 = nc.dram_tensor("dbg16", [P, 16, TT], BF16, kind="ExternalOutput").ap()

    S = Sched()
    es = ExitStack()
    with es:
        def sb(name, shape, dt):
            return es.enter_context(nc.sbuf_tensor(name, shape, dt))

        cv = sb("cv", [P, NCV], F32)
        dm = sb("dm", [P, H, P], F32)
        xi = sb("xi", [P, H, P], F32)
        cst = sb("cst", [P, 2, TT], F32)
        ident = sb("ident", [P, P], BF16)
        ones = sb("ones", [P, P], BF16)
        bvb = sb("bvb", [1, 2048], BF16)
        res = [sb("res0", [P, 8, TT], F32), sb("res1", [P, 8, TT], F32)]
        xbf = sb("xbf", [P, 8, TT], BF16)
        Sst = sb("Sst", [P, H, DV], F32)
        Sbf = sb("Sbf", [P, H, DV], BF16)
        ring = sb("ring", [P, NS, SLOT], BF16)
        mean = sb("mean", [P, TT], F32)
        rT = sb("rT", [P, 2, TT], BF16)
        var = sb("var", [P, TT], F32)
        rstd = sb("rstd", [P, TT], F32)
        identf = rstd[:, 0:P]
        msq = rstd
        st6 = sb("st6", [P, 4, 6], F32)
        mv = sb("mv", [P, 4, 2], F32)
        gsd = sb("gsd", [P, 4], F32)
        grs = sb("grs", [P, 4], F32)
        gnm = sb("gnm", [P, 4], F32)
        UW = 7168 + 4096 + 8 * (TT + HALO) + 2048 + 1536 + 1024 + 1024 + 256 + 256 + 2048
        U = sb("U", [P, UW], F32)
        o = 0

        def carve(nwords, dt, shape3=None):
            nonlocal o
            v = U[:, o:o + nwords]
            o += nwords
            if dt == BF16:
                v = v.bitcast(BF16)
            if shape3 is not None:
                v = v.rearrange("p (a b) -> p a b", b=shape3)
            return v

        A0 = o
        act = carve(NFC * TT // 2, BF16, TT)
        sgt = carve(3 * TT, F32, TT)
        o = A0
        acc = carve(8 * TT, F32, TT)
        tmp = carve(6 * TT, F32, TT)
        assert o == 7168
        o = 7168
        sgate = carve(16 * TT // 2, BF16, TT)
        o = 7168
        ybfA = carve(8 * TT // 2, BF16, TT)
        ysqA = carve(8 * TT // 2, BF16, TT)
        assert o == 7168 + 4096
        ubuf = carve(8 * (TT + HALO) // 2, BF16, TT + HALO)
        dg = carve(NPE * P // 2, BF16, P)
        usbf = carve(8 * TT // 2, BF16, TT)
        ybfB = usbf
        qk = carve(6 * TT // 2, BF16, TT)
        vbf = carve(2 * 4 * DV // 2, BF16, DV)
        rtok = carve(2 * 4 * DV // 2, BF16, DV)
        scb = carve(4 * P // 2, BF16, P)
        kzb = carve(4 * P // 2, BF16, P)
        merged = carve(8 * TT // 2, BF16, TT)
        ysqB = merged
        assert o <= UW, (o, UW)
        ps = es.enter_context(nc.psum_tensor("ps", [P, NB, TT], F32))
        pst = es.enter_context(nc.psum_tensor("pst", [P, 2, 1024], BF16))

        cvc = lambda name, i=0: cv[:, CV[name] + i:CV[name] + i + 1]

        bank_ctr = [0]

        held = set()

        def bank():
            while True:
                b = bank_ctr[0]
                bank_ctr[0] = (b + 1) % NB
                if b not in held:
                    return b

        kt_ctr = [0]

        CAST_AHEAD = 10

        def cgroup(i):
            return i if i < CAST_AHEAD + 2 else CAST_AHEAD + 2 + (i - CAST_AHEAD - 2) // 7

        def issue_cast(i):
            S.add("pool", (lambda i: lambda e: e.dma_start(out=wbf[i], in_=wall[i]))(i),
                  writes=[("wbf", i), ("casttok", i % 8)], dma=("cast", cgroup(i)), final=True)

        S.add("sp", lambda e: e.dma_start(out=cv[:, 0:NCV_IN], in_=cvec[:, 0:NCV_IN]), writes=["cv"], dma="const", final=True)
        S.add("sp", lambda e: e.dma_start(out=dm[:], in_=dmd), writes=["dm"], dma="const", final=True)
        S.add("sp", lambda e: e.dma_start(out=xi[:], in_=xid), writes=["xi"], dma="const", final=True)
        S.add("pool", lambda e: e.dma_start(out=bvb[:], in_=bvrow), writes=["bvb"], dma="constp", final=True)
        S.add("pool", lambda e: e.memset(identf, 0.0), writes=["identf"])
        S.add("pool", lambda e: e.affine_select(out=identf, in_=identf, pattern=[[-1, P]], compare_op=ALU.not_equal,
                                                fill=1.0, base=0, channel_multiplier=1), reads=["identf"], writes=["identf"])
        S.add("pool", lambda e: e.tensor_copy(out=ident[:], in_=identf), reads=["identf"], writes=["ident"])
        S.add("pool", lambda e: e.memset(ones[:], 1.0), writes=["ones"])
        S.add("pool", lambda e: e.memset(Sst[:], 0.0), writes=[("S", h) for h in range(H)])
        S.add("pool", lambda e: e.memset(Sbf[:], 0.0), writes=[("Sbf", h) for h in range(H)])
        S.add("pool", lambda e: e.memset(ubuf[:], 0.0), writes=[("u", cc) for cc in range(8)])
        S.add("pool", lambda e: e.memset(cv[:, CV["eps"]:CV["eps"] + 1], EPS), writes=["cv_eps"])
        S.add("pool", lambda e: e.tensor_scalar(out=cv[:, CV["A"]:CV["A"] + 32], in0=cv[:, CV["ln1_g"]:CV["ln1_g"] + 32],
                                                scalar1=ALPHA, scalar2=None, op0=ALU.mult), reads=["cv"], writes=["cvA"])
        S.add("pool", lambda e: e.tensor_scalar(out=cv[:, CV["Y"]:CV["Y"] + 24], in0=cv[:, CV["bb"]:CV["bb"] + 24],
                                                scalar1=0.5, scalar2=None, op0=ALU.mult), reads=["cv"], writes=["cvY"])
        S.add("pool", lambda e: e.tensor_scalar(out=cv[:, CV["Z"]:CV["Z"] + 8 * CW], in0=cv[:, CV["conv_k"]:CV["conv_k"] + 8 * CW],
                                                scalar1=0.5, scalar2=None, op0=ALU.mult), reads=["cv"], writes=["cvZ"])
        CONSTS = ["cv", "cv_eps", "cvA", "cvY", "cvZ"]
        for i in range(CAST_AHEAD):
            issue_cast(i)

        wstate = {"g": 0}

        def load_piece(t, i, nel=SLOT):
            g = wstate["g"]
            wstate["g"] = g + 1
            slot = g % NS
            S.add("sp", (lambda slot, i, nel: lambda e: e.dma_start(out=ring[:, slot, 0:nel], in_=wbf[i, :, 0:nel]))(slot, i, nel),
                  reads=[("wbf", i)], writes=[("wslot", slot)], dma=("w", slot))
            if t == 0 and i + CAST_AHEAD < PIECES_PER_TILE:
                issue_cast(i + CAST_AHEAD)
            return slot


        def wv(slot, off, kcn, width):
            return ring[:, slot, off:off + kcn * width].rearrange("p (k w) -> p k w", w=width)

        def mm_group(pairs, out_ap):
            def fn(e):
                n = len(pairs)
                inst = None
                for i, (l, r) in enumerate(pairs):
                    inst = e.matmul(out_ap, lhsT=l, rhs=r, start=(i == 0), stop=(i == n - 1))
                return inst
            return fn

        def mm_fine(pairs, out_ap, reads_each, common_reads, writes):
            n = len(pairs)
            for i, (l, r) in enumerate(pairs):
                S.add("pe", (lambda l, r, i: lambda e: e.matmul(out_ap, lhsT=l, rhs=r, start=(i == 0), stop=(i == n - 1)))(l, r, i),
                      reads=common_reads + reads_each[i], writes=writes)

        def act_fn(out, in_, func, bias=0.0, scale=1.0):
            return lambda e: e.activation(out=out, in_=in_, func=func, bias=bias, scale=scale)

        def stt_fn(out, in0, scalar, in1, op0, op1):
            return lambda e: e.scalar_tensor_tensor(out=out, in0=in0, scalar=scalar, in1=in1, op0=op0, op1=op1)

        def tt_fn(out, in0, in1, op):
            return lambda e: e.tensor_tensor(out=out, in0=in0, in1=in1, op=op)

        def ts_fn(out, in0, s1, s2, op0, op1=None):
            if op1 is None:
                return lambda e: e.tensor_scalar(out=out, in0=in0, scalar1=s1, scalar2=None, op0=op0)
            return lambda e: e.tensor_scalar(out=out, in0=in0, scalar1=s1, scalar2=s2, op0=op0, op1=op1)

        def cp_fn(out, in_):
            return lambda e: e.tensor_copy(out=out, in_=in_)

        NORM_DVE = (0, 1, 2, 3, 4)

        def stats_begin():
            b1, b2 = bank(), bank()
            held.add(b1)
            held.add(b2)
            return (b1, b2)

        def stats_prep(src, srckey, ybf, ysq, ykey, kc):
            S.add("dve", cp_fn(ybf[:, kc, :], src[:, kc, :]), reads=[(srckey, kc)], writes=[(ykey, 0, kc)])
            S.add("act", act_fn(ysq[:, kc, :], src[:, kc, :], AF.Square), reads=[(srckey, kc)], writes=[(ykey, 1, kc)])

        def stats_mm(pre, ybf, ysq, ykey, kc):
            b1, b2 = pre
            S.add("pe", (lambda kc: lambda e: e.matmul(ps[:, b1, :], lhsT=ones[:], rhs=ybf[:, kc, :], start=(kc == 0), stop=(kc == 7)))(kc),
                  reads=[(ykey, 0, kc), "ones"], writes=[("ps", b1)])
            S.add("pe", (lambda kc: lambda e: e.matmul(ps[:, b2, :], lhsT=ones[:], rhs=ysq[:, kc, :], start=(kc == 0), stop=(kc == 7)))(kc),
                  reads=[(ykey, 1, kc), "ones"], writes=[("ps", b2)])

        def stats_chunk(pre, src, srckey, ybf, ysq, ykey, kc):
            stats_prep(src, srckey, ybf, ysq, ykey, kc)
            stats_mm(pre, ybf, ysq, ykey, kc)

        def layer_norm(src, srckey, ybf, ysq, ykey, gname, bname, g2name, b2name, out_bf, out_bf_key, silu=False, pre=None):
            if pre is None:
                pre = stats_begin()
                for kc in range(8):
                    stats_chunk(pre, src, srckey, ybf, ysq, ykey, kc)
            b1, b2 = pre
            S.add("act", act_fn(msq[:], ps[:, b1, :], AF.Square, scale=1.0 / D), reads=[("ps", b1)], writes=["rstd"])
            S.add("act", act_fn(mean[:], ps[:, b1, :], AF.Copy, scale=1.0 / D), reads=[("ps", b1)], writes=["mean"])
            S.add("dve", stt_fn(var[:], ps[:, b2, :], 1.0 / D, msq[:], ALU.mult, ALU.subtract),
                  reads=[("ps", b2), "rstd"], writes=["var"])
            held.discard(b1)
            held.discard(b2)
            S.add("act", act_fn(var[:], var[:], AF.Sqrt, bias=cvc("eps")), reads=["var", "cv_eps"], writes=["var"])
            NH = 2
            mb = mean[:].unsqueeze(1).to_broadcast([P, NH, TT])
            rb = rstd[:].unsqueeze(1).to_broadcast([P, NH, TT])
            POOL_PAIR = -1
            for hf in range(0, 8, NH):
                keys = [(srckey, kc) for kc in range(hf, hf + NH)]
                eng = "pool" if hf == POOL_PAIR else "dve"
                S.add(eng, tt_fn(src[:, hf:hf + NH, :], src[:, hf:hf + NH, :], mb, ALU.subtract), reads=keys + ["mean"], writes=keys)
                if hf == 0:
                    S.add("dve", lambda e: e.reciprocal(out=rstd[:], in_=var[:]), reads=["var"], writes=["rstd"])
            for hf in range(0, 8, NH):
                keys = [(srckey, kc) for kc in range(hf, hf + NH)]
                eng = "pool" if hf == POOL_PAIR else "dve"
                S.add(eng, tt_fn(src[:, hf:hf + NH, :], src[:, hf:hf + NH, :], rb, ALU.mult), reads=keys + ["rstd"], writes=keys)
                for kc in range(hf, hf + NH):
                    if out_bf is not None:
                        S.add("act", act_fn(out_bf[:, kc, :], src[:, kc, :], AF.Silu if silu else AF.Identity,
                                            bias=cvc(bname, kc), scale=cvc(gname, kc)),
                              reads=[(srckey, kc)] + CONSTS, writes=[(out_bf_key, kc)])
            if g2name is not None:
                for kc in range(8):
                    S.add("act", act_fn(src[:, kc, :], src[:, kc, :], AF.Identity, bias=cvc(b2name, kc), scale=cvc(g2name, kc)),
                          reads=[(srckey, kc)] + CONSTS, writes=[(srckey, kc)])

        def ffn_gu(t, pbase, i0, i1):
            for i in range(i0, i1):
                slot = load_piece(t, pbase + i)
                for j in range(2):
                    fc = 2 * i + j
                    wg_ = wv(slot, (2 * j) * 1024, 8, P)
                    wu_ = wv(slot, (2 * j + 1) * 1024, 8, P)
                    bg_, bu_ = bank(), bank()
                    if fc == 0:
                        mm_fine([(wg_[:, kc, :], xbf[:, kc, :]) for kc in range(8)], ps[:, bg_, :],
                                [[("xbf", kc)] for kc in range(8)], [("wslot", slot)], [("ps", bg_)])
                        mm_fine([(wu_[:, kc, :], xbf[:, kc, :]) for kc in range(8)], ps[:, bu_, :],
                                [[("xbf", kc)] for kc in range(8)], [("wslot", slot)], [("ps", bu_)])
                    else:
                        S.add("pe", mm_group([(wg_[:, kc, :], xbf[:, kc, :]) for kc in range(8)], ps[:, bg_, :]),
                              reads=[("wslot", slot)] + [("xbf", kc) for kc in range(8)], writes=[("ps", bg_)])
                        S.add("pe", mm_group([(wu_[:, kc, :], xbf[:, kc, :]) for kc in range(8)], ps[:, bu_, :]),
                              reads=[("wslot", slot)] + [("xbf", kc) for kc in range(8)], writes=[("ps", bu_)])
                    sg = sgt[:, fc % 3, :]
                    S.add("act", act_fn(sg, ps[:, bg_, :], AF.Silu), reads=[("ps", bg_)], writes=[("sgt", fc % 3)])
                    S.add("dve", stt_fn(act[:, fc, :], sg, 0.5, ps[:, bu_, :], ALU.mult, ALU.mult),
                          reads=[("sgt", fc % 3), ("ps", bu_)], writes=[("act", fc)])

        def ffn_down(t, buf, pbase, first):
            R = res[buf]
            rk = ("res", buf)
            pre = stats_begin()
            for oc in range(8):
                slot = load_piece(t, pbase + NFC // 2 + oc, nel=NFC * P)
                wd_ = wv(slot, 0, NFC, P)
                b = bank()
                S.add("pe", mm_group([(wd_[:, fc, :], act[:, fc, :]) for fc in range(NFC)], ps[:, b, :]),
                      reads=[("wslot", slot)] + [("act", fc) for fc in range(NFC)], writes=[("ps", b)])
                if first:
                    S.add("dve", stt_fn(R[:, oc, :], R[:, oc, :], ALPHA, ps[:, b, :], ALU.mult, ALU.add),
                          reads=[(rk, oc), ("ps", b)], writes=[(rk, oc)])
                else:
                    S.add("dve", tt_fn(R[:, oc, :], R[:, oc, :], ps[:, b, :], ALU.add),
                          reads=[(rk, oc), ("ps", b)], writes=[(rk, oc)])
                stats_prep(R, rk, ybfA, ysqA, "yA", oc)
                if oc > 0:
                    stats_mm(pre, ybfA, ysqA, "yA", oc - 1)
            stats_mm(pre, ybfA, ysqA, "yA", 7)
            return pre

        def proj_items(t, h):
            hb = h % 2
            st = {}

            def grp(w, dst_bank_key=None):
                b = bank()
                S.add("pe", mm_group([(w[:, kc, :], xbf[:, kc, :]) for kc in range(8)], ps[:, b, :]),
                      reads=[("wslot", st["cur"])] + [("xbf", kc) for kc in range(8)], writes=[("ps", b)])
                return b

            def it_rot(which):
                def f():
                    if which == 0:
                        st["sa"] = load_piece(t, PI_HEAD + 2 * h)
                    sa = st["sa"]
                    st["cur"] = sa
                    b = grp(wv(sa, which * 1024, 8, P))
                    bname, bpname = (("bq", "bqp"), ("bk", "bkp"))[which]
                    ta, tb = 2 * which, 2 * which + 1
                    S.add("dve", stt_fn(tmp[:, ta, :], ps[:, b, :], cvc(bname, h), cst[:, 0, :], ALU.add, ALU.mult),
                          reads=[("ps", b), "cst"] + CONSTS, writes=[("tmp", ta)])
                    S.add("dve", stt_fn(tmp[0:64, tb, :], ps[64:128, b, :], cv[0:64, CV[bpname] + h:CV[bpname] + h + 1],
                                        cst[0:64, 1, :], ALU.add, ALU.mult),
                          reads=[("ps", b), "cst"] + CONSTS, writes=[("tmp", tb, 0)])
                    S.add("dve", stt_fn(tmp[64:128, tb, :], ps[0:64, b, :], cv[64:128, CV[bpname] + h:CV[bpname] + h + 1],
                                        cst[64:128, 1, :], ALU.add, ALU.mult),
                          reads=[("ps", b), "cst"] + CONSTS, writes=[("tmp", tb, 1)])
                    dst = qk[:, hb * 3 + which, :]
                    dkey = ("q", hb) if which == 0 else ("k", hb)
                    S.add("pool", tt_fn(dst, tmp[:, ta, :], tmp[:, tb, :], ALU.add),
                          reads=[("tmp", ta), ("tmp", tb, 0), ("tmp", tb, 1)], writes=[dkey])
                    if which == 0:
                        xv = xi[:, h, :].unsqueeze(1).to_broadcast([P, 4, P])
                        qv = qk[:, hb * 3 + 0, :].rearrange("p (c i) -> p c i", i=P)
                        qxv = qk[:, hb * 3 + 2, :].rearrange("p (c i) -> p c i", i=P)
                        S.add("pool", tt_fn(qxv, qv, xv, ALU.mult), reads=[("q", hb), "xi"], writes=[("qx", hb)])
                return f

            def it_v(c):
                def f():
                    if c == 0:
                        st["sb"] = load_piece(t, PI_HEAD + 2 * h + 1, nel=8 * DV)
                    sb_ = st["sb"]
                    wv_ = wv(sb_, 0, 8, DV)
                    b = bank()
                    pairs = [(xbf[:, kc, c * P:(c + 1) * P], wv_[:, kc, :]) for kc in range(8)]
                    pairs.append((ones[0:1, :], bvb[0:1, h * DV:(h + 1) * DV]))
                    S.add("pe", mm_group(pairs, ps[:, b, 0:DV]),
                          reads=[("wslot", sb_), "ones", "bvb"] + [("xbf", kc) for kc in range(8)], writes=[("ps", b)])
                    S.add("act", act_fn(vbf[:, hb * 4 + c, :], ps[:, b, 0:DV], AF.Copy), reads=[("ps", b)], writes=[("v", hb, c)])
                return f

            def it_g(j):
                def f():
                    sa = st["sa"]
                    st["cur"] = sa
                    b = grp(wv(sa, 2048 + j * 1024, 8, P))
                    S.add("act", act_fn(sgate[:, 2 * h + j, :], ps[:, b, :], AF.Silu, bias=cvc("bg", 2 * h + j)),
                          reads=[("ps", b)] + CONSTS, writes=[("sgate", 2 * h + j)])
                return f

            return [it_rot(0), it_rot(1), it_g(0), it_g(1), it_v(0), it_v(1), it_v(2), it_v(3)]

        def retention(t, h, nxt, fill):
            hb = h % 2

            def pop():
                if nxt:
                    nxt.pop(0)()

            qs = lambda c: qk[:, hb * 3 + 0, c * P:(c + 1) * P]
            ks_ = lambda c: qk[:, hb * 3 + 1, c * P:(c + 1) * P]
            qxs = lambda c: qk[:, hb * 3 + 2, c * P:(c + 1) * P]
            vs = lambda c: vbf[:, hb * 4 + c, :]
            for c in range(4):
                bs_ = bank()
                S.add("pe", mm_group([(ks_(c), qs(c))], ps[:, bs_, 0:P]), reads=[("k", hb), ("q", hb)], writes=[("ps", bs_)])
                S.add("dve", tt_fn(scb[:, c, :], ps[:, bs_, 0:P], dm[:, h, :], ALU.mult),
                      reads=[("ps", bs_), "dm"], writes=[("scb", c)])
                fill(1)
                S.add("pe", (lambda c: lambda e: e.transpose(pst[:, 0, 0:P], ks_(c), ident[:]))(c),
                      reads=[("k", hb), "ident"], writes=[("pstb", 0)])
                S.add("act", act_fn(kzb[:, c, :], pst[:, 0, 0:P], AF.Copy, scale=cvc("zeta", h)),
                      reads=[("pstb", 0)] + CONSTS, writes=[("kzb", c)])
                pop()
            for c in range(4):
                bo = bank()
                S.add("pe", mm_group([(scb[:, c, :], vs(c)), (qxs(c), Sbf[:, h, :])], ps[:, bo, 0:DV]),
                      reads=[("scb", c), ("v", hb, c), ("qx", hb), ("Sbf", h)], writes=[("ps", bo)])
                bd = bank()
                S.add("pe", mm_group([(kzb[:, c, :], vs(c))], ps[:, bd, 0:DV]), reads=[("kzb", c), ("v", hb, c)], writes=[("ps", bd)])
                S.add("dve", stt_fn(Sst[:, h, :], Sst[:, h, :], gC[h], ps[:, bd, 0:DV], ALU.mult, ALU.add),
                      reads=[("S", h), ("ps", bd)], writes=[("S", h)])
                S.add("act", act_fn(Sbf[:, h, :], Sst[:, h, :], AF.Copy), reads=[("S", h)], writes=[("Sbf", h)])
                S.add("dve", (lambda c, bo: lambda e: e.bn_stats(out=st6[:, c, :], in_=ps[:, bo, 0:DV]))(c, bo),
                      reads=[("ps", bo)], writes=[("st6", c)])
                S.add("dve", (lambda c: lambda e: e.bn_aggr(out=mv[:, c, :], in_=st6[:, c, :]))(c),
                      reads=[("st6", c)], writes=[("mv", c)])
                S.add("act", act_fn(gsd[:, c:c + 1], mv[:, c, 1:2], AF.Sqrt, bias=cvc("eps")),
                      reads=[("mv", c), "cv_eps"], writes=[("gsd", c)])
                pop()
                fill(1)
                S.add("dve", (lambda c: lambda e: e.reciprocal(out=grs[:, c:c + 1], in_=gsd[:, c:c + 1]))(c),
                      reads=[("gsd", c)], writes=[("grs", c)])
                S.add("dve", stt_fn(gnm[:, c:c + 1], mv[:, c, 0:1], -1.0, grs[:, c:c + 1], ALU.mult, ALU.mult),
                      reads=[("mv", c), ("grs", c)], writes=[("gnm", c)])
                fill(1)
                S.add("act", act_fn(rtok[:, hb * 4 + c, :], ps[:, bo, 0:DV], AF.Identity,
                                    bias=gnm[:, c:c + 1], scale=grs[:, c:c + 1]),
                      reads=[("ps", bo), ("grs", c), ("gnm", c)], writes=[("rtok", hb, c)])
            while nxt:
                pop()
                fill(1)

            def trf(e):
                inst = None
                for eb in range(2):
                    for c in range(4):
                        inst = e.transpose(pst[:, 1, eb * TT + c * P:eb * TT + (c + 1) * P],
                                           rtok[:, hb * 4 + c, eb * P:(eb + 1) * P], ident[:])
                return inst
            S.add("pe", trf, reads=[("rtok", hb, c) for c in range(4)] + ["ident"], writes=[("pstb", 1)])
            for eb in range(2):
                ec = 2 * h + eb
                S.add("act", act_fn(rT[:, eb, :], pst[:, 1, eb * TT:(eb + 1) * TT], AF.Copy, scale=cvc("gn_g", ec)),
                      reads=[("pstb", 1)] + CONSTS, writes=[("rT", eb)])
                S.add("pool", tt_fn(sgate[:, ec, :], rT[:, eb, :], sgate[:, ec, :], ALU.mult),
                      reads=[("rT", eb), ("sgate", ec)], writes=[("sgate", ec)])

        def glu_conv(t):
            dg_load(0)
            for i in range(4):
                if i == 1:
                    pe_evac(0)
                if i == 3:
                    pe_evac(1)
                slot = load_piece(t, PI_GLU + i)
                for j in range(2):
                    cc = 2 * i + j
                    wa_ = wv(slot, (2 * j) * 1024, 8, P)
                    wb_ = wv(slot, (2 * j + 1) * 1024, 8, P)
                    ba_, bb_ = bank(), bank()
                    if cc == 0:
                        mm_fine([(wa_[:, kc, :], xbf[:, kc, :]) for kc in range(8)], ps[:, ba_, :],
                                [[("xbf", kc)] for kc in range(8)], [("wslot", slot)], [("ps", ba_)])
                        mm_fine([(wb_[:, kc, :], xbf[:, kc, :]) for kc in range(8)], ps[:, bb_, :],
                                [[("xbf", kc)] for kc in range(8)], [("wslot", slot)], [("ps", bb_)])
                    else:
                        S.add("pe", mm_group([(wa_[:, kc, :], xbf[:, kc, :]) for kc in range(8)], ps[:, ba_, :]),
                              reads=[("wslot", slot)] + [("xbf", kc) for kc in range(8)], writes=[("ps", ba_)])
                        S.add("pe", mm_group([(wb_[:, kc, :], xbf[:, kc, :]) for kc in range(8)], ps[:, bb_, :]),
                              reads=[("wslot", slot)] + [("xbf", kc) for kc in range(8)], writes=[("ps", bb_)])
                    ta, tb = 4, 5
                    S.add("act", act_fn(tmp[:, ta, :], ps[:, ba_, :], AF.Identity, bias=cvc("ba", cc)),
                          reads=[("ps", ba_)] + CONSTS, writes=[("tmp", ta)])
                    S.add("act", act_fn(tmp[:, tb, :], ps[:, bb_, :], AF.Tanh, bias=cvc("Y", cc), scale=0.5),
                          reads=[("ps", bb_)] + CONSTS, writes=[("tmp", tb)])
                    S.add("pool", cp_fn(ubuf[:, cc, 0:HALO], ubuf[:, cc, TT:TT + HALO]), reads=[("u", cc)], writes=[("u", cc)])
                    S.add("dve", stt_fn(ubuf[:, cc, HALO:HALO + TT], tmp[:, tb, :], 1.0, tmp[:, ta, :], ALU.add, ALU.mult),
                          reads=[("tmp", ta), ("tmp", tb), ("u", cc)], writes=[("u", cc)])
            return None

        def dg_load(cc):
            S.add("pool", (lambda cc: lambda e: e.dma_start(out=dg.rearrange("p k j -> p (k j)"), in_=dgs[cc]))(cc),
                  reads=[("dgs", cc)], writes=[("dg", k) for k in range(NPE)], dma="dgld")

        def pe_evac(cc):
            b = bank()
            S.add("pe", mm_group([(dg[:, k, :], ubuf[:, cc, k:k + TT]) for k in range(NPE)], ps[:, b, :]),
                  reads=[("u", cc)] + [("dg", k) for k in range(NPE)], writes=[("ps", b)])
            S.add("act", act_fn(acc[:, cc, :], ps[:, b, :], AF.Identity, bias=cvc("conv_b", cc)),
                  reads=[("ps", b)] + CONSTS, writes=[("acc", cc)])
            if cc + 1 < 8:
                dg_load(cc + 1)

        def conv_slots():
            def dve_tap(cc, j):
                return lambda: S.add(
                    "dve", stt_fn(acc[:, cc, :], ubuf[:, cc, j:j + TT], cvc("Z", cc * CW + j), acc[:, cc, :], ALU.mult, ALU.add),
                    reads=[("u", cc), ("acc", cc)] + CONSTS, writes=[("acc", cc)])
            slots = []
            for p in range(1, 5):
                prev = (2 * p - 2, 2 * p - 1)
                taps = [dve_tap(cc, j) for j in range(NPE, CW) for cc in prev]
                half = len(taps) // 2
                if p < 4:
                    slots.append((lambda c_: lambda: pe_evac(c_))(2 * p))
                slots += taps[:half]
                if p < 4:
                    slots.append((lambda c_: lambda: pe_evac(c_))(2 * p + 1))
                slots += taps[half:]
            return slots

        def conv_tail(t):
            layer_norm(acc, "acc", ybfB, ysqB, "yB", "cln_g", "cln_b", None, None, usbf, "usbf", silu=True)

        def convo_items(t):
            cslots = {}

            def item(oc):
                def f():
                    if oc % 4 == 0:
                        cslots[oc // 4] = load_piece(t, PI_CONVO + oc // 4)
                    sc_ = cslots[oc // 4]
                    wc_ = wv(sc_, (oc % 4) * 1024, 8, P)
                    bc_ = bank()
                    S.add("pe", mm_group([(wc_[:, kc, :], usbf[:, kc, :]) for kc in range(8)], ps[:, bc_, :]),
                          reads=[("wslot", sc_)] + [("usbf", kc) for kc in range(8)], writes=[("ps", bc_)])
                    S.add("act", act_fn(acc[:, oc, :], ps[:, bc_, :], AF.Copy), reads=[("ps", bc_)], writes=[("acc", oc)])
                return f
            return [item(oc) for oc in range(8)]

        def merge_out(t, buf):
            R = res[buf]
            rk = ("res", buf)
            for oc in range(8):
                sm = load_piece(t, PI_MERGE + oc)
                wgr_ = wv(sm, 0, 8, P)
                wgc_ = wv(sm, 1024, 8, P)
                wro_ = wv(sm, 2048, 16, P)
                b1, b2, b3 = bank(), bank(), bank()
                S.add("pe", mm_group([(wgr_[:, kc, :], xbf[:, kc, :]) for kc in range(8)], ps[:, b1, :]),
                      reads=[("wslot", sm)] + [("xbf", kc) for kc in range(8)], writes=[("ps", b1)])
                S.add("pe", mm_group([(wgc_[:, kc, :], xbf[:, kc, :]) for kc in range(8)], ps[:, b2, :]),
                      reads=[("wslot", sm)] + [("xbf", kc) for kc in range(8)], writes=[("ps", b2)])
                mm_fine([(wro_[:, ec, :], sgate[:, ec, :]) for ec in range(16)], ps[:, b3, :],
                        [[("sgate", ec)] for ec in range(16)], [("wslot", sm)], [("ps", b3)])
                S.add("act", act_fn(tmp[:, 0, :], ps[:, b1, :], AF.Tanh, bias=cvc("Y", 8 + oc), scale=0.5),
                      reads=[("ps", b1)] + CONSTS, writes=[("tmp", 0)])
                S.add("act", act_fn(tmp[:, 1, :], ps[:, b2, :], AF.Tanh, bias=cvc("Y", 16 + oc), scale=0.5),
                      reads=[("ps", b2)] + CONSTS, writes=[("tmp", 1)])
                S.add("dve", stt_fn(tmp[:, 2, :], tmp[:, 0, :], 1.0, ps[:, b3, :], ALU.add, ALU.mult),
                      reads=[("tmp", 0), ("ps", b3)], writes=[("tmp", 2)])
                S.add("dve", stt_fn(tmp[:, 3, :], tmp[:, 1, :], 1.0, acc[:, oc, :], ALU.add, ALU.mult),
                      reads=[("tmp", 1), ("acc", oc)], writes=[("tmp", 3)])
                S.add("pool", tt_fn(merged[:, oc, :], tmp[:, 2, :], tmp[:, 3, :], ALU.add),
                      reads=[("tmp", 2), ("tmp", 3)], writes=[("merged", oc)])
            oslots = {}
            pre = stats_begin()
            for oc in range(8):
                if oc % 4 == 0:
                    oslots[oc // 4] = load_piece(t, PI_OUT + oc // 4)
                so = oslots[oc // 4]
                wo_ = wv(so, (oc % 4) * 1024, 8, P)
                b = bank()
                S.add("pe", mm_group([(wo_[:, kc, :], merged[:, kc, :]) for kc in range(8)], ps[:, b, :]),
                      reads=[("wslot", so)] + [("merged", kc) for kc in range(8)], writes=[("ps", b)])
                S.add("dve", stt_fn(R[:, oc, :], ps[:, b, :], 0.5, R[:, oc, :], ALU.mult, ALU.add),
                      reads=[(rk, oc), ("ps", b)], writes=[(rk, oc)])
                stats_prep(R, rk, ybfA, ysqA, "yA", oc)
                if oc > 0:
                    stats_mm(pre, ybfA, ysqA, "yA", oc - 1)
            stats_mm(pre, ybfA, ysqA, "yA", 7)
            return pre

        def dump(src, keys):
            S.add("act", lambda e: e.dma_start(out=dbg_out, in_=src), reads=keys, dma="dbg")

        def dump16(src, keys, n):
            S.add("act", lambda e: e.dma_start(out=dbg16[:, 0:n, :], in_=src), reads=keys, dma="dbg")

        def load_x(t):
            buf = t % 2
            S.add("sp", (lambda t, buf: lambda e: e.dma_start(out=res[buf][:], in_=xT[t]))(t, buf),
                  writes=[(("res", buf), kc) for kc in range(8)], dma=("xin", buf))

        GU_EARLY = 2
        def build_diags():
            for cc in range(8):
                for k in range(NPE):
                    S.add("dve", ts_fn(dg[:, k, :], ident[:], cvc("Z", cc * CW + k), None, ALU.mult),
                          reads=["ident"] + CONSTS, writes=[("dg", k)])
                S.add("pool", (lambda cc: lambda e: e.dma_start(out=dgs[cc], in_=dg.rearrange("p k j -> p (k j)")))(cc),
                      reads=[("dg", k) for k in range(NPE)], writes=[("dgs", cc)], dma="dgst")

        def xcast(t):
            b_ = t % 2
            for kc in range(8):
                S.add("dve", cp_fn(xbf[:, kc, :], res[b_][:, kc, :]), reads=[(("res", b_), kc)], writes=[("xbf", kc)])

        load_x(0)
        for t in range(ntiles):
            buf = t % 2
            R = res[buf]
            rk = ("res", buf)
            S.add("sp", (lambda t: lambda e: e.dma_start(out=cst[:], in_=csd[t]))(t), writes=["cst"], dma="cst")
            if t == 0:
                xcast(0)
            def finish():
                S.add("act", (lambda t, buf: lambda e: e.dma_start(out=outT[t], in_=res[buf][:]))(t, buf),
                      reads=[(rk, kc) for kc in range(8)], dma=("out", buf))
            if stop == "pro":
                finish(); continue
            ffn_gu(t, PI_FFN1, 0 if t == 0 else GU_EARLY, NFC // 2)
            if t == 0:
                build_diags()
            pre1 = ffn_down(t, buf, PI_FFN1, first=True)
            layer_norm(R, rk, ybfA, ysqA, "yA", "ln1_g", "ln1_b", "A_g1", "A_b1", xbf, "xbf", pre=pre1)
            if stop == "ln1":
                finish(); continue
            if dbg == "ln1" and t == ntiles - 1:
                dump(R[:], [(rk, kc) for kc in range(8)])
            glu_conv(t)
            cops = conv_slots()
            cpos = 0
            if stop == "glu":
                for c_ in cops:
                    c_()
                finish(); continue
            items = proj_items(t, 0)
            if stop is not None and stop.startswith("p_"):
                for it in items[:int(stop[2:])]:
                    it()
                finish(); continue
            for it in items:
                it()

            def fill(n):
                nonlocal cpos
                for _ in range(n):
                    if cpos < len(cops):
                        cops[cpos]()
                        cpos += 1

            conv_done = False
            import os as _os
            for h in range(int(_os.environ.get("KHEADS", H))):
                nxt = proj_items(t, h + 1) if h + 1 < H else None
                if nxt is None:
                    while cpos < len(cops):
                        fill(8)
                    if not conv_done:
                        conv_tail(t)
                        conv_done = True
                    nxt = convo_items(t)
                if _os.environ.get("KNOFILL"):
                    retention(t, h, nxt, lambda n: None)
                else:
                    retention(t, h, nxt, fill)
                if cpos >= len(cops) and not conv_done:
                    conv_tail(t)
                    conv_done = True
            while cpos < len(cops):
                fill(8)
            if not conv_done:
                conv_tail(t)
            if stop == "heads":
                finish(); continue
            if dbg == "rg" and t == ntiles - 1:
                dump16(sgate, [("sgate", ec) for ec in range(16)], 16)
            if dbg == "us" and t == ntiles - 1:
                dump16(usbf, [("usbf", kc) for kc in range(8)], 8)
            if stop == "convln":
                finish(); continue
            if t + 1 < ntiles:
                load_x(t + 1)
            pre2 = merge_out(t, buf)
            if dbg == "merged" and t == ntiles - 1:
                dump16(merged, [("merged", kc) for kc in range(8)], 8)
            if dbg == "y2" and t == ntiles - 1:
                dump(R[:], [(rk, kc) for kc in range(8)])
            if stop == "merge":
                finish(); continue
            layer_norm(R, rk, ybfA, ysqA, "yA", "ln2_g", "ln2_b", "A_g2", "A_b2", xbf, "xbf", pre=pre2)
            ffn_gu(t, PI_FFN2, 0, NFC // 2)
            if t + 1 < ntiles:
                xcast(t + 1)
            pre3 = ffn_down(t, buf, PI_FFN2, first=False)
            if t + 1 < ntiles:
                ffn_gu(t + 1, PI_FFN1, 0, GU_EARLY)
            layer_norm(R, rk, ybfA, ysqA, "yA", None, None, "ln3_g", "ln3_b", None, None, pre=pre3)
            S.add("act", (lambda t, buf: lambda e: e.dma_start(out=outT[t], in_=res[buf][:]))(t, buf),
                  reads=[(rk, kc) for kc in range(8)], dma=("out", buf))

        S.finalize()
        sems = {}
        for k in S.sem_keys():
            nm = "s_" + "".join(ch if ch.isalnum() else "_" for ch in str(k))
            sems[k] = es.enter_context(nc.semaphore(nm))
        with nc.Block() as block:
            @block.tensor
            def _(e):
                S.run_engine("pe", e, sems)

            @block.scalar
            def _(e):
                S.run_engine("act", e, sems)
                for k, c in S.dma_count.items():
                    if isinstance(k, tuple) and k[0] == "out" or k == "dbg":
                        e.wait_ge(sems[("dma", k)], 16 * c)

            @block.vector
            def _(e):
                S.run_engine("dve", e, sems)

            @block.gpsimd
            def _(e):
                S.run_engine("pool", e, sems)

            @block.sync
            def _(e):
                S.run_engine("sp", e, sems)
    return nc


def _prep_inputs(inputs):
    inp = {k: np.asarray(v) for k, v in inputs.items()}
    wall = _build_wall(inp)
    cv, bv, cs, dm, xit, _ = _build_consts(inp)
    x = np.asarray(inp["x"], np.float32)
    in_maps = []
    for b in range(x.shape[0]):
        xT = np.ascontiguousarray(x[b].reshape(NT, TT, 8, P).transpose(0, 3, 2, 1))
        in_maps.append({"xT": xT, "wall": wall, "cvec": cv, "bvrow": bv, "cs": cs, "dmask": dm, "xit": xit})
    return in_maps


def kernel(**inputs):
    in_maps = _prep_inputs(inputs)
    nc = build_program(NT)
    res = run_bass_kernel_spmd(nc, in_maps, core_ids=list(range(len(in_maps))))
    outs = []
    for r in res.results:
        oT = np.asarray(r["outT"], np.float32)
        outs.append(oT.transpose(0, 3, 2, 1).reshape(SEQ, D))
    return np.stack(outs, 0).astype(np.float32)
```

```python
from contextlib import ExitStack
import numpy as np
import concourse.bass as bass
import concourse.mybir as mybir
from concourse.bass_utils import run_bass_kernel_spmd

F32 = mybir.dt.float32
BF16 = mybir.dt.bfloat16
AF = mybir.ActivationFunctionType
ALU = mybir.AluOpType

P = 128
D = 1024
SEQ = 4096
TT = 512
NT = SEQ // TT
DFF = 2816
NFC = DFF // P
H = 8
DV = 256
CW = 31
HALO = CW - 1
ALPHA = float(2.0 ** 0.25)
EPS = 1e-5
NS = 4
SLOT = 4096
NB = 6
NPE = 23


class Op:
    __slots__ = ("eng", "fn", "deps", "signaled", "tok", "idx", "dma_key", "is_dma", "gidx", "dma_val", "final")


class Sched:
    ENGS = ("pe", "act", "dve", "pool", "sp")
    WINDOW = 10 ** 9

    def __init__(self):
        self.ops = {e: [] for e in self.ENGS}
        self.last_writer = {}
        self.readers = {}
        self.dma_count = {}
        self.n = 0

    def add(self, eng, fn, reads=(), writes=(), dma=None, final=False):
        o = Op()
        o.eng = eng
        o.fn = fn
        o.signaled = False
        o.is_dma = dma is not None
        o.dma_key = dma
        o.gidx = self.n
        o.tok = 0
        o.dma_val = 0
        o.final = final
        self.n += 1
        deps = {}
        for r in reads:
            w = self.last_writer.get(r)
            if w is not None:
                deps[w.gidx] = w
        for k in writes:
            w = self.last_writer.get(k)
            if w is not None:
                deps[w.gidx] = w
            rd = self.readers.get(k)
            if rd:
                for x in rd.values():
                    deps[x.gidx] = x
        o.deps = list(deps.values())
        for d in o.deps:
            if not d.is_dma:
                d.signaled = True
        rk = ("dma", o.gidx) if o.is_dma else eng
        for r in reads:
            self.readers.setdefault(r, {})[rk] = o
        for k in writes:
            self.last_writer[k] = o
            self.readers[k] = {}
        if o.is_dma:
            c = self.dma_count.get(dma, 0) + 1
            self.dma_count[dma] = c
            o.dma_val = 16 * c
        o.idx = len(self.ops[eng])
        self.ops[eng].append(o)
        return o

    def finalize(self):
        for e in self.ENGS:
            c = 0
            for o in self.ops[e]:
                if o.is_dma:
                    if o.final:
                        o.dma_val = 16 * self.dma_count[o.dma_key]
                elif o.signaled:
                    c += 1
                    o.tok = c

    def sem_keys(self):
        return list(self.ENGS) + [("dma", k) for k in self.dma_count]

    def run_engine(self, e, engobj, sems):
        known = {}
        for o in self.ops[e]:
            for d in o.deps:
                if d.is_dma:
                    sk = ("dma", d.dma_key)
                    val = d.dma_val
                else:
                    if d.eng == e and (e == "pe" or (o.idx - d.idx) > self.WINDOW):
                        continue
                    sk = d.eng
                    val = d.tok
                if known.get(sk, 0) >= val:
                    continue
                engobj.wait_ge(sems[sk], val)
                known[sk] = val
            inst = o.fn(engobj)
            if o.is_dma:
                inst.then_inc(sems[("dma", o.dma_key)], 16)
            elif o.signaled:
                inst.then_inc(sems[e], 1)


CV = {}
_c = 0
for _n, _w in (("ln1_g", 8), ("ln1_b", 8), ("ln2_g", 8), ("ln2_b", 8), ("ln3_g", 8), ("ln3_b", 8),
               ("bq", 8), ("bqp", 8), ("bk", 8), ("bkp", 8), ("bg", 16), ("ba", 8),
               ("bb", 8), ("bgr", 8), ("bgc", 8),
               ("gn_g", 16), ("conv_b", 8), ("cln_g", 8), ("cln_b", 8), ("conv_k", 8 * CW), ("zeta", 8),
               ("A", 32), ("Y", 24), ("Z", 8 * CW), ("eps", 1)):
    CV[_n] = _c
    _c += _w
NCV = _c
NCV_IN = CV["A"]
CV["A_g1"] = CV["A"]
CV["A_b1"] = CV["A"] + 8
CV["A_g2"] = CV["A"] + 16
CV["A_b2"] = CV["A"] + 24


def _blk(W, c0, width=128):
    K = W.shape[0]
    return np.ascontiguousarray(W[:, c0:c0 + width].reshape(K // P, P, width).transpose(1, 0, 2)).reshape(P, -1)


def _piece(parts):
    out = np.zeros((P, SLOT), np.float32)
    o = 0
    for a in parts:
        out[:, o:o + a.shape[1]] = a
        o += a.shape[1]
    return out


def _ffn_pieces(wg, wu, wd):
    ps = []
    for i in range(NFC // 2):
        ps.append(_piece([_blk(wg, (2 * i) * P), _blk(wu, (2 * i) * P), _blk(wg, (2 * i + 1) * P), _blk(wu, (2 * i + 1) * P)]))
    for oc in range(8):
        ps.append(_piece([_blk(wd, oc * P)]))
    return ps


PIECES_PER_TILE = 19 + 4 + 16 + 2 + 8 + 2 + 19
PI_FFN1, PI_GLU, PI_HEAD, PI_CONVO, PI_MERGE, PI_OUT, PI_FFN2 = 0, 19, 23, 39, 41, 49, 51


def _build_wall(inp):
    w_in = inp["w_in"][0]
    perm = np.arange(1024).reshape(H, 2, 64)[:, ::-1, :].reshape(-1)
    Wq, Wk = w_in[:, 0:1024], w_in[:, 1024:2048]
    Wv, Wg = w_in[:, 2048:4096], w_in[:, 4096:6144]
    Wa, Wb = w_in[:, 6144:7168], w_in[:, 7168:8192]
    Wgr, Wgc = w_in[:, 8192:9216], w_in[:, 9216:10240]
    ps = []
    ps += _ffn_pieces(inp["ffn1_w_gate"][0], inp["ffn1_w_up"][0], inp["ffn1_w_down"][0])
    for i in range(4):
        ps.append(_piece([_blk(Wa, (2 * i) * P), _blk(Wb, (2 * i) * P), _blk(Wa, (2 * i + 1) * P), _blk(Wb, (2 * i + 1) * P)]))
    for h in range(H):
        ps.append(_piece([_blk(Wq, h * P), _blk(Wk, h * P), _blk(Wg, (2 * h) * P), _blk(Wg, (2 * h + 1) * P)]))
        ps.append(_piece([_blk(Wv, h * DV, DV)]))
    wco = inp["w_conv_o"][0]
    for i in range(2):
        ps.append(_piece([_blk(wco, (4 * i + j) * P) for j in range(4)]))
    wro = inp["w_ret_o"][0]
    for oc in range(8):
        ps.append(_piece([_blk(Wgr, oc * P), _blk(Wgc, oc * P), _blk(wro, oc * P)]))
    wo = inp["w_out"][0]
    for i in range(2):
        ps.append(_piece([_blk(wo, (4 * i + j) * P) for j in range(4)]))
    ps += _ffn_pieces(inp["ffn2_w_gate"][0], inp["ffn2_w_up"][0], inp["ffn2_w_down"][0])
    assert len(ps) == PIECES_PER_TILE
    return np.stack(ps, 0)


def _cols(v):
    return np.ascontiguousarray(np.asarray(v, np.float32).reshape(-1, P).T)


def _build_consts(inp):
    cv = np.zeros((P, NCV), np.float32)

    def put(name, v):
        c = _cols(v)
        cv[:, CV[name]:CV[name] + c.shape[1]] = c

    for n in ("ln1_g", "ln1_b", "ln2_g", "ln2_b", "ln3_g", "ln3_b"):
        put(n, inp[n][0])
    b_in = np.asarray(inp["b_in"][0], np.float32)
    perm = np.arange(1024).reshape(H, 2, 64)[:, ::-1, :].reshape(-1)
    put("bq", b_in[0:1024]); put("bqp", b_in[0:1024][perm])
    put("bk", b_in[1024:2048]); put("bkp", b_in[1024:2048][perm])
    put("bg", b_in[4096:6144]); put("ba", b_in[6144:7168]); put("bb", b_in[7168:8192])
    put("bgr", b_in[8192:9216]); put("bgc", b_in[9216:10240])
    put("gn_g", inp["ret_gn_g"][0]); put("conv_b", inp["conv_b"][0])
    put("cln_g", inp["conv_ln_g"][0]); put("cln_b", inp["conv_ln_b"][0])
    ck = np.asarray(inp["conv_k"][0], np.float32).reshape(CW, 8, P)
    cv[:, CV["conv_k"]:CV["conv_k"] + 8 * CW] = ck.transpose(2, 1, 0).reshape(P, 8 * CW)
    bv = np.ascontiguousarray(b_in[2048:4096].reshape(1, 2048))
    half = 64
    freqs = (np.float32(10000.0) ** (-np.arange(half, dtype=np.float32) / np.float32(half))).astype(np.float32)
    ang = (np.arange(SEQ, dtype=np.float32)[:, None] * freqs[None, :]).astype(np.float32)
    cos = np.cos(ang).astype(np.float32).T
    sin = np.sin(ang).astype(np.float32).T
    ctab = np.concatenate([cos, cos], 0)
    stab = np.concatenate([-sin, sin], 0)
    cs = np.stack([ctab, stab], 1).reshape(P, 2, NT, TT).transpose(2, 0, 1, 3)
    hh = np.arange(H, dtype=np.float32)
    log_g = np.log(1.0 - np.exp2(-5.0 - hh)).astype(np.float32)
    idx = np.arange(P, dtype=np.float32)
    diff = idx[None, :] - idx[:, None]
    sc = np.float32(P ** -0.5)
    dm = np.where(diff[:, None, :] >= 0, np.exp(np.maximum(diff, 0.0)[:, None, :] * log_g[None, :, None]), 0.0) * sc
    xi = np.exp((idx[None, :] + 1.0) * log_g[:, None]) * sc
    xit = np.broadcast_to(xi[None], (P, H, P))
    zeta = np.exp((P - 1.0 - idx)[:, None] * log_g[None, :])
    cv[:, CV["zeta"]:CV["zeta"] + 8] = zeta
    gC = [float(np.exp(np.float32(P) * log_g[h])) for h in range(H)]
    return (cv, bv, np.ascontiguousarray(cs, np.float32), np.ascontiguousarray(dm, np.float32),
            np.ascontiguousarray(xit, np.float32), gC)


def _gc_consts():
    hh = np.arange(H, dtype=np.float32)
    log_g = np.log(1.0 - np.exp2(-5.0 - hh)).astype(np.float32)
    return [float(np.exp(np.float32(P) * log_g[h])) for h in range(H)]


def build_program(ntiles=NT, dbg=None, stop=None):
    gC = _gc_consts()
    nc = bass.Bass("TRN2", target_bir_lowering=False)
    xT = nc.dram_tensor("xT", [NT, P, 8, TT], F32, kind="ExternalInput").ap()
    wall = nc.dram_tensor("wall", [PIECES_PER_TILE, P, SLOT], F32, kind="ExternalInput").ap()
    cvec = nc.dram_tensor("cvec", [P, NCV], F32, kind="ExternalInput").ap()
    bvrow = nc.dram_tensor("bvrow", [1, 2048], F32, kind="ExternalInput").ap()
    csd = nc.dram_tensor("cs", [NT, P, 2, TT], F32, kind="ExternalInput").ap()
    dmd = nc.dram_tensor("dmask", [P, H, P], F32, kind="ExternalInput").ap()
    xid = nc.dram_tensor("xit", [P, H, P], F32, kind="ExternalInput").ap()
    outT = nc.dram_tensor("outT", [NT, P, 8, TT], F32, kind="ExternalOutput").ap()
    wbf = nc.dram_tensor("wbf", [PIECES_PER_TILE, P, SLOT], BF16).ap()
    dgs = nc.dram_tensor("dgs", [8, P, NPE * P], BF16).ap()
    dbg_out = None
    if dbg is not None:
        dbg_out = nc.dram_tensor("dbg", [P, 8, TT], F32, kind="ExternalOutput").ap()
        dbg16 = nc.dram_tensor("dbg16", [P, 16, TT], BF16, kind="ExternalOutput").ap()

    S = Sched()
    es = ExitStack()
    with es:
        def sb(name, shape, dt):
            return es.enter_context(nc.sbuf_tensor(name, shape, dt))

        cv = sb("cv", [P, NCV], F32)
        dm = sb("dm", [P, H, P], F32)
        xi = sb("xi", [P, H, P], F32)
        cst = sb("cst", [P, 2, TT], F32)
        ident = sb("ident", [P, P], BF16)
        ones = sb("ones", [P, P], BF16)
        bvb = sb("bvb", [1, 2048], BF16)
        res = [sb("res0", [P, 8, TT], F32), sb("res1", [P, 8, TT], F32)]
        xbf = sb("xbf", [P, 8, TT], BF16)
        Sst = sb("Sst", [P, H, DV], F32)
        Sbf = sb("Sbf", [P, H, DV], BF16)
        ring = sb("ring", [P, NS, SLOT], BF16)
        mean = sb("mean", [P, TT], F32)
        rT = sb("rT", [P, 2, TT], BF16)
        var = sb("var", [P, TT], F32)
        rstd = sb("rstd", [P, TT], F32)
        identf = rstd[:, 0:P]
        msq = rstd
        st6 = sb("st6", [P, 4, 6], F32)
        mv = sb("mv", [P, 4, 2], F32)
        gsd = sb("gsd", [P, 4], F32)
        grs = sb("grs", [P, 4], F32)
        gnm = sb("gnm", [P, 4], F32)
        UW = 7168 + 4096 + 8 * (TT + HALO) + 2048 + 1536 + 1024 + 1024 + 256 + 256 + 2048
        U = sb("U", [P, UW], F32)
        o = 0

        def carve(nwords, dt, shape3=None):
            nonlocal o
            v = U[:, o:o + nwords]
            o += nwords
            if dt == BF16:
                v = v.bitcast(BF16)
            if shape3 is not None:
                v = v.rearrange("p (a b) -> p a b", b=shape3)
            return v

        A0 = o
        act = carve(NFC * TT // 2, BF16, TT)
        sgt = carve(3 * TT, F32, TT)
        o = A0
        acc = carve(8 * TT, F32, TT)
        tmp = carve(6 * TT, F32, TT)
        assert o == 7168
        o = 7168
        sgate = carve(16 * TT // 2, BF16, TT)
        o = 7168
        ybfA = carve(8 * TT // 2, BF16, TT)
        ysqA = carve(8 * TT // 2, BF16, TT)
        assert o == 7168 + 4096
        ubuf = carve(8 * (TT + HALO) // 2, BF16, TT + HALO)
        dg = carve(NPE * P // 2, BF16, P)
        usbf = carve(8 * TT // 2, BF16, TT)
        ybfB = usbf
        qk = carve(6 * TT // 2, BF16, TT)
        vbf = carve(2 * 4 * DV // 2, BF16, DV)
        rtok = carve(2 * 4 * DV // 2, BF16, DV)
        scb = carve(4 * P // 2, BF16, P)
        kzb = carve(4 * P // 2, BF16, P)
        merged = carve(8 * TT // 2, BF16, TT)
        ysqB = merged
        assert o <= UW, (o, UW)
        ps = es.enter_context(nc.psum_tensor("ps", [P, NB, TT], F32))
        pst = es.enter_context(nc.psum_tensor("pst", [P, 2, 1024], BF16))

        cvc = lambda name, i=0: cv[:, CV[name] + i:CV[name] + i + 1]

        bank_ctr = [0]

        held = set()

        def bank():
            while True:
                b = bank_ctr[0]
                bank_ctr[0] = (b + 1) % NB
                if b not in held:
                    return b

        kt_ctr = [0]

        S.add("sp", lambda e: e.dma_start(out=cv[:, 0:NCV_IN], in_=cvec[:, 0:NCV_IN]), writes=["cv"], dma="const", final=True)
        S.add("sp", lambda e: e.dma_start(out=dm[:], in_=dmd), writes=["dm"], dma="const", final=True)
        S.add("sp", lambda e: e.dma_start(out=xi[:], in_=xid), writes=["xi"], dma="const", final=True)
        S.add("pool", lambda e: e.dma_start(out=bvb[:], in_=bvrow), writes=["bvb"], dma="constp", final=True)
        S.add("pool", lambda e: e.memset(identf, 0.0), writes=["identf"])
        S.add("pool", lambda e: e.affine_select(out=identf, in_=identf, pattern=[[-1, P]], compare_op=ALU.not_equal,
                                                fill=1.0, base=0, channel_multiplier=1), reads=["identf"], writes=["identf"])
        S.add("pool", lambda e: e.tensor_copy(out=ident[:], in_=identf), reads=["identf"], writes=["ident"])
        S.add("pool", lambda e: e.memset(ones[:], 1.0), writes=["ones"])
        S.add("pool", lambda e: e.memset(Sst[:], 0.0), writes=[("S", h) for h in range(H)])
        S.add("pool", lambda e: e.memset(Sbf[:], 0.0), writes=[("Sbf", h) for h in range(H)])
        S.add("pool", lambda e: e.memset(ubuf[:], 0.0), writes=[("u", cc) for cc in range(8)])
        S.add("pool", lambda e: e.memset(cv[:, CV["eps"]:CV["eps"] + 1], EPS), writes=["cv_eps"])
        S.add("pool", lambda e: e.tensor_scalar(out=cv[:, CV["A"]:CV["A"] + 32], in0=cv[:, CV["ln1_g"]:CV["ln1_g"] + 32],
                                                scalar1=ALPHA, scalar2=None, op0=ALU.mult), reads=["cv"], writes=["cvA"])
        S.add("pool", lambda e: e.tensor_scalar(out=cv[:, CV["Y"]:CV["Y"] + 24], in0=cv[:, CV["bb"]:CV["bb"] + 24],
                                                scalar1=0.5, scalar2=None, op0=ALU.mult), reads=["cv"], writes=["cvY"])
        S.add("pool", lambda e: e.tensor_scalar(out=cv[:, CV["Z"]:CV["Z"] + 8 * CW], in0=cv[:, CV["conv_k"]:CV["conv_k"] + 8 * CW],
                                                scalar1=0.5, scalar2=None, op0=ALU.mult), reads=["cv"], writes=["cvZ"])
        CONSTS = ["cv", "cv_eps", "cvA", "cvY", "cvZ"]
        PF = 3

        def piece_nel(i):
            if PI_FFN1 + NFC // 2 <= i < PI_GLU or i >= PI_FFN2 + NFC // 2:
                return NFC * P
            if PI_HEAD <= i < PI_CONVO and (i - PI_HEAD) % 2 == 1:
                return 8 * DV
            return SLOT

        def t0_fetch(i):
            slot = i % NS
            nel = piece_nel(i)
            S.add("pool", (lambda slot, i, nel: lambda e: e.dma_start(out=ring[:, slot, 0:nel], in_=wall[i, :, 0:nel]))(slot, i, nel),
                  writes=[("wslot", slot)], dma=("w0", slot))

        for i in range(PF):
            t0_fetch(i)

        wstate = {"g": 0}

        def load_piece(t, i, nel=SLOT):
            g = wstate["g"]
            wstate["g"] = g + 1
            slot = g % NS
            assert nel == piece_nel(i) and g % PIECES_PER_TILE == i, (t, i, nel, g)
            if g < PIECES_PER_TILE:
                S.add("sp", (lambda slot, i, nel: lambda e: e.dma_start(out=wbf[i, :, 0:nel], in_=ring[:, slot, 0:nel]))(slot, i, nel),
                      reads=[("wslot", slot)], writes=[("wbf", i)], dma=("wst", slot))
                if i + PF < PIECES_PER_TILE:
                    t0_fetch(i + PF)
            else:
                S.add("sp", (lambda slot, i, nel: lambda e: e.dma_start(out=ring[:, slot, 0:nel], in_=wbf[i, :, 0:nel]))(slot, i, nel),
                      reads=[("wbf", i)], writes=[("wslot", slot)], dma=("w", slot))
            return slot


        def wv(slot, off, kcn, width):
            return ring[:, slot, off:off + kcn * width].rearrange("p (k w) -> p k w", w=width)

        def mm_group(pairs, out_ap):
            def fn(e):
                n = len(pairs)
                inst = None
                for i, (l, r) in enumerate(pairs):
                    inst = e.matmul(out_ap, lhsT=l, rhs=r, start=(i == 0), stop=(i == n - 1))
                return inst
            return fn

        def act_fn(out, in_, func, bias=0.0, scale=1.0):
            return lambda e: e.activation(out=out, in_=in_, func=func, bias=bias, scale=scale)

        def stt_fn(out, in0, scalar, in1, op0, op1):
            return lambda e: e.scalar_tensor_tensor(out=out, in0=in0, scalar=scalar, in1=in1, op0=op0, op1=op1)

        def tt_fn(out, in0, in1, op):
            return lambda e: e.tensor_tensor(out=out, in0=in0, in1=in1, op=op)

        def ts_fn(out, in0, s1, s2, op0, op1=None):
            if op1 is None:
                return lambda e: e.tensor_scalar(out=out, in0=in0, scalar1=s1, scalar2=None, op0=op0)
            return lambda e: e.tensor_scalar(out=out, in0=in0, scalar1=s1, scalar2=s2, op0=op0, op1=op1)

        def cp_fn(out, in_):
            return lambda e: e.tensor_copy(out=out, in_=in_)

        NORM_DVE = (0, 1, 2, 3, 4)

        def stats_begin():
            b1, b2 = bank(), bank()
            held.add(b1)
            held.add(b2)
            return (b1, b2)

        def stats_prep(src, srckey, ybf, ysq, ykey, kc):
            S.add("dve", cp_fn(ybf[:, kc, :], src[:, kc, :]), reads=[(srckey, kc)], writes=[(ykey, 0, kc)])
            S.add("act", act_fn(ysq[:, kc, :], src[:, kc, :], AF.Square), reads=[(srckey, kc)], writes=[(ykey, 1, kc)])

        def stats_mm(pre, ybf, ysq, ykey, kc):
            b1, b2 = pre
            S.add("pe", (lambda kc: lambda e: e.matmul(ps[:, b1, :], lhsT=ones[:], rhs=ybf[:, kc, :], start=(kc == 0), stop=(kc == 7)))(kc),
                  reads=[(ykey, 0, kc), "ones"], writes=[("ps", b1)])
            S.add("pe", (lambda kc: lambda e: e.matmul(ps[:, b2, :], lhsT=ones[:], rhs=ysq[:, kc, :], start=(kc == 0), stop=(kc == 7)))(kc),
                  reads=[(ykey, 1, kc), "ones"], writes=[("ps", b2)])

        def stats_chunk(pre, src, srckey, ybf, ysq, ykey, kc):
            stats_prep(src, srckey, ybf, ysq, ykey, kc)
            stats_mm(pre, ybf, ysq, ykey, kc)

        def layer_norm(src, srckey, ybf, ysq, ykey, gname, bname, g2name, b2name, out_bf, out_bf_key, silu=False, pre=None):
            if pre is None:
                pre = stats_begin()
                for kc in range(8):
                    stats_chunk(pre, src, srckey, ybf, ysq, ykey, kc)
            b1, b2 = pre
            S.add("act", act_fn(msq[:], ps[:, b1, :], AF.Square, scale=1.0 / D), reads=[("ps", b1)], writes=["rstd"])
            S.add("act", act_fn(mean[:], ps[:, b1, :], AF.Copy, scale=1.0 / D), reads=[("ps", b1)], writes=["mean"])
            S.add("dve", stt_fn(var[:], ps[:, b2, :], 1.0 / D, msq[:], ALU.mult, ALU.subtract),
                  reads=[("ps", b2), "rstd"], writes=["var"])
            held.discard(b1)
            held.discard(b2)
            S.add("act", act_fn(var[:], var[:], AF.Sqrt, bias=cvc("eps")), reads=["var", "cv_eps"], writes=["var"])
            NH = 2
            mb = mean[:].unsqueeze(1).to_broadcast([P, NH, TT])
            rb = rstd[:].unsqueeze(1).to_broadcast([P, NH, TT])
            for hf in range(0, 8, NH):
                keys = [(srckey, kc) for kc in range(hf, hf + NH)]
                S.add("dve", tt_fn(src[:, hf:hf + NH, :], src[:, hf:hf + NH, :], mb, ALU.subtract), reads=keys + ["mean"], writes=keys)
            S.add("dve", lambda e: e.reciprocal(out=rstd[:], in_=var[:]), reads=["var"], writes=["rstd"])
            for hf in range(0, 8, NH):
                keys = [(srckey, kc) for kc in range(hf, hf + NH)]
                S.add("dve", tt_fn(src[:, hf:hf + NH, :], src[:, hf:hf + NH, :], rb, ALU.mult), reads=keys + ["rstd"], writes=keys)
                for kc in range(hf, hf + NH):
                    if out_bf is not None:
                        S.add("act", act_fn(out_bf[:, kc, :], src[:, kc, :], AF.Silu if silu else AF.Identity,
                                            bias=cvc(bname, kc), scale=cvc(gname, kc)),
                              reads=[(srckey, kc)] + CONSTS, writes=[(out_bf_key, kc)])
            if g2name is not None:
                for kc in range(8):
                    S.add("act", act_fn(src[:, kc, :], src[:, kc, :], AF.Identity, bias=cvc(b2name, kc), scale=cvc(g2name, kc)),
                          reads=[(srckey, kc)] + CONSTS, writes=[(srckey, kc)])

        def ffn_gu(t, pbase, i0, i1):
            for i in range(i0, i1):
                slot = load_piece(t, pbase + i)
                for j in range(2):
                    fc = 2 * i + j
                    wg_ = wv(slot, (2 * j) * 1024, 8, P)
                    wu_ = wv(slot, (2 * j + 1) * 1024, 8, P)
                    bg_, bu_ = bank(), bank()
                    S.add("pe", mm_group([(wg_[:, kc, :], xbf[:, kc, :]) for kc in range(8)], ps[:, bg_, :]),
                          reads=[("wslot", slot)] + [("xbf", kc) for kc in range(8)], writes=[("ps", bg_)])
                    S.add("pe", mm_group([(wu_[:, kc, :], xbf[:, kc, :]) for kc in range(8)], ps[:, bu_, :]),
                          reads=[("wslot", slot)] + [("xbf", kc) for kc in range(8)], writes=[("ps", bu_)])
                    sg = sgt[:, fc % 3, :]
                    S.add("act", act_fn(sg, ps[:, bg_, :], AF.Silu), reads=[("ps", bg_)], writes=[("sgt", fc % 3)])
                    S.add("dve", stt_fn(act[:, fc, :], sg, 0.5, ps[:, bu_, :], ALU.mult, ALU.mult),
                          reads=[("sgt", fc % 3), ("ps", bu_)], writes=[("act", fc)])

        def ffn_down(t, buf, pbase, first):
            R = res[buf]
            rk = ("res", buf)
            pre = stats_begin()
            for oc in range(8):
                slot = load_piece(t, pbase + NFC // 2 + oc, nel=NFC * P)
                wd_ = wv(slot, 0, NFC, P)
                b = bank()
                S.add("pe", mm_group([(wd_[:, fc, :], act[:, fc, :]) for fc in range(NFC)], ps[:, b, :]),
                      reads=[("wslot", slot)] + [("act", fc) for fc in range(NFC)], writes=[("ps", b)])
                if first:
                    S.add("dve", stt_fn(R[:, oc, :], R[:, oc, :], ALPHA, ps[:, b, :], ALU.mult, ALU.add),
                          reads=[(rk, oc), ("ps", b)], writes=[(rk, oc)])
                else:
                    S.add("dve", tt_fn(R[:, oc, :], R[:, oc, :], ps[:, b, :], ALU.add),
                          reads=[(rk, oc), ("ps", b)], writes=[(rk, oc)])
                stats_prep(R, rk, ybfA, ysqA, "yA", oc)
                if oc > 0:
                    stats_mm(pre, ybfA, ysqA, "yA", oc - 1)
            stats_mm(pre, ybfA, ysqA, "yA", 7)
            return pre

        def proj_items(t, h):
            hb = h % 2
            st = {}

            def grp(w, dst_bank_key=None):
                b = bank()
                S.add("pe", mm_group([(w[:, kc, :], xbf[:, kc, :]) for kc in range(8)], ps[:, b, :]),
                      reads=[("wslot", st["cur"])] + [("xbf", kc) for kc in range(8)], writes=[("ps", b)])
                return b

            def it_rot(which):
                def f():
                    if which == 0:
                        st["sa"] = load_piece(t, PI_HEAD + 2 * h)
                    sa = st["sa"]
                    st["cur"] = sa
                    b = grp(wv(sa, which * 1024, 8, P))
                    bname, bpname = (("bq", "bqp"), ("bk", "bkp"))[which]
                    ta, tb = 2 * which, 2 * which + 1
                    S.add("dve", stt_fn(tmp[:, ta, :], ps[:, b, :], cvc(bname, h), cst[:, 0, :], ALU.add, ALU.mult),
                          reads=[("ps", b), "cst"] + CONSTS, writes=[("tmp", ta)])
                    S.add("dve", stt_fn(tmp[0:64, tb, :], ps[64:128, b, :], cv[0:64, CV[bpname] + h:CV[bpname] + h + 1],
                                        cst[0:64, 1, :], ALU.add, ALU.mult),
                          reads=[("ps", b), "cst"] + CONSTS, writes=[("tmp", tb, 0)])
                    S.add("dve", stt_fn(tmp[64:128, tb, :], ps[0:64, b, :], cv[64:128, CV[bpname] + h:CV[bpname] + h + 1],
                                        cst[64:128, 1, :], ALU.add, ALU.mult),
                          reads=[("ps", b), "cst"] + CONSTS, writes=[("tmp", tb, 1)])
                    dst = qk[:, hb * 3 + which, :]
                    dkey = ("q", hb) if which == 0 else ("k", hb)
                    S.add("pool", tt_fn(dst, tmp[:, ta, :], tmp[:, tb, :], ALU.add),
                          reads=[("tmp", ta), ("tmp", tb, 0), ("tmp", tb, 1)], writes=[dkey])
                    if which == 0:
                        xv = xi[:, h, :].unsqueeze(1).to_broadcast([P, 4, P])
                        qv = qk[:, hb * 3 + 0, :].rearrange("p (c i) -> p c i", i=P)
                        qxv = qk[:, hb * 3 + 2, :].rearrange("p (c i) -> p c i", i=P)
                        S.add("pool", tt_fn(qxv, qv, xv, ALU.mult), reads=[("q", hb), "xi"], writes=[("qx", hb)])
                return f

            def it_v(c):
                def f():
                    if c == 0:
                        st["sb"] = load_piece(t, PI_HEAD + 2 * h + 1, nel=8 * DV)
                    sb_ = st["sb"]
                    wv_ = wv(sb_, 0, 8, DV)
                    b = bank()
                    pairs = [(xbf[:, kc, c * P:(c + 1) * P], wv_[:, kc, :]) for kc in range(8)]
                    pairs.append((ones[0:1, :], bvb[0:1, h * DV:(h + 1) * DV]))
                    S.add("pe", mm_group(pairs, ps[:, b, 0:DV]),
                          reads=[("wslot", sb_), "ones", "bvb"] + [("xbf", kc) for kc in range(8)], writes=[("ps", b)])
                    S.add("act", act_fn(vbf[:, hb * 4 + c, :], ps[:, b, 0:DV], AF.Copy), reads=[("ps", b)], writes=[("v", hb, c)])
                return f

            def it_g(j):
                def f():
                    sa = st["sa"]
                    st["cur"] = sa
                    b = grp(wv(sa, 2048 + j * 1024, 8, P))
                    S.add("act", act_fn(sgate[:, 2 * h + j, :], ps[:, b, :], AF.Silu, bias=cvc("bg", 2 * h + j)),
                          reads=[("ps", b)] + CONSTS, writes=[("sgate", 2 * h + j)])
                return f

            return [it_rot(0), it_rot(1), it_g(0), it_g(1), it_v(0), it_v(1), it_v(2), it_v(3)]

        def retention(t, h, nxt, fill):
            hb = h % 2

            def pop():
                if nxt:
                    nxt.pop(0)()

            qs = lambda c: qk[:, hb * 3 + 0, c * P:(c + 1) * P]
            ks_ = lambda c: qk[:, hb * 3 + 1, c * P:(c + 1) * P]
            qxs = lambda c: qk[:, hb * 3 + 2, c * P:(c + 1) * P]
            vs = lambda c: vbf[:, hb * 4 + c, :]
            for c in range(4):
                bs_ = bank()
                S.add("pe", mm_group([(ks_(c), qs(c))], ps[:, bs_, 0:P]), reads=[("k", hb), ("q", hb)], writes=[("ps", bs_)])
                S.add("dve", tt_fn(scb[:, c, :], ps[:, bs_, 0:P], dm[:, h, :], ALU.mult),
                      reads=[("ps", bs_), "dm"], writes=[("scb", c)])
                fill(1)
                S.add("pe", (lambda c: lambda e: e.transpose(pst[:, 0, 0:P], ks_(c), ident[:]))(c),
                      reads=[("k", hb), "ident"], writes=[("pstb", 0)])
                S.add("act", act_fn(kzb[:, c, :], pst[:, 0, 0:P], AF.Copy, scale=cvc("zeta", h)),
                      reads=[("pstb", 0)] + CONSTS, writes=[("kzb", c)])
                pop()
            for c in range(4):
                bo = bank()
                S.add("pe", mm_group([(scb[:, c, :], vs(c)), (qxs(c), Sbf[:, h, :])], ps[:, bo, 0:DV]),
                      reads=[("scb", c), ("v", hb, c), ("qx", hb), ("Sbf", h)], writes=[("ps", bo)])
                bd = bank()
                S.add("pe", mm_group([(kzb[:, c, :], vs(c))], ps[:, bd, 0:DV]), reads=[("kzb", c), ("v", hb, c)], writes=[("ps", bd)])
                S.add("dve", stt_fn(Sst[:, h, :], Sst[:, h, :], gC[h], ps[:, bd, 0:DV], ALU.mult, ALU.add),
                      reads=[("S", h), ("ps", bd)], writes=[("S", h)])
                S.add("act", act_fn(Sbf[:, h, :], Sst[:, h, :], AF.Copy), reads=[("S", h)], writes=[("Sbf", h)])
                S.add("dve", (lambda c, bo: lambda e: e.bn_stats(out=st6[:, c, :], in_=ps[:, bo, 0:DV]))(c, bo),
                      reads=[("ps", bo)], writes=[("st6", c)])
                S.add("dve", (lambda c: lambda e: e.bn_aggr(out=mv[:, c, :], in_=st6[:, c, :]))(c),
                      reads=[("st6", c)], writes=[("mv", c)])
                S.add("act", act_fn(gsd[:, c:c + 1], mv[:, c, 1:2], AF.Sqrt, bias=cvc("eps")),
                      reads=[("mv", c), "cv_eps"], writes=[("gsd", c)])
                pop()
                fill(1)
                S.add("dve", (lambda c: lambda e: e.reciprocal(out=grs[:, c:c + 1], in_=gsd[:, c:c + 1]))(c),
                      reads=[("gsd", c)], writes=[("grs", c)])
                S.add("dve", stt_fn(gnm[:, c:c + 1], mv[:, c, 0:1], -1.0, grs[:, c:c + 1], ALU.mult, ALU.mult),
                      reads=[("mv", c), ("grs", c)], writes=[("gnm", c)])
                fill(1)
                S.add("act", act_fn(rtok[:, hb * 4 + c, :], ps[:, bo, 0:DV], AF.Identity,
                                    bias=gnm[:, c:c + 1], scale=grs[:, c:c + 1]),
                      reads=[("ps", bo), ("grs", c), ("gnm", c)], writes=[("rtok", hb, c)])
            while nxt:
                pop()
                fill(1)

            def trf(e):
                inst = None
                for eb in range(2):
                    for c in range(4):
                        inst = e.transpose(pst[:, 1, eb * TT + c * P:eb * TT + (c + 1) * P],
                                           rtok[:, hb * 4 + c, eb * P:(eb + 1) * P], ident[:])
                return inst
            S.add("pe", trf, reads=[("rtok", hb, c) for c in range(4)] + ["ident"], writes=[("pstb", 1)])
            for eb in range(2):
                ec = 2 * h + eb
                S.add("act", act_fn(rT[:, eb, :], pst[:, 1, eb * TT:(eb + 1) * TT], AF.Copy, scale=cvc("gn_g", ec)),
                      reads=[("pstb", 1)] + CONSTS, writes=[("rT", eb)])
                S.add("pool", tt_fn(sgate[:, ec, :], rT[:, eb, :], sgate[:, ec, :], ALU.mult),
                      reads=[("rT", eb), ("sgate", ec)], writes=[("sgate", ec)])

        def glu_conv(t):
            dg_load(0)
            for i in range(4):
                if i == 1:
                    pe_evac(0)
                if i == 3:
                    pe_evac(1)
                slot = load_piece(t, PI_GLU + i)
                for j in range(2):
                    cc = 2 * i + j
                    wa_ = wv(slot, (2 * j) * 1024, 8, P)
                    wb_ = wv(slot, (2 * j + 1) * 1024, 8, P)
                    ba_, bb_ = bank(), bank()
                    S.add("pe", mm_group([(wa_[:, kc, :], xbf[:, kc, :]) for kc in range(8)], ps[:, ba_, :]),
                          reads=[("wslot", slot)] + [("xbf", kc) for kc in range(8)], writes=[("ps", ba_)])
                    S.add("pe", mm_group([(wb_[:, kc, :], xbf[:, kc, :]) for kc in range(8)], ps[:, bb_, :]),
                          reads=[("wslot", slot)] + [("xbf", kc) for kc in range(8)], writes=[("ps", bb_)])
                    ta, tb = 4, 5
                    S.add("act", act_fn(tmp[:, ta, :], ps[:, ba_, :], AF.Identity, bias=cvc("ba", cc)),
                          reads=[("ps", ba_)] + CONSTS, writes=[("tmp", ta)])
                    S.add("act", act_fn(tmp[:, tb, :], ps[:, bb_, :], AF.Tanh, bias=cvc("Y", cc), scale=0.5),
                          reads=[("ps", bb_)] + CONSTS, writes=[("tmp", tb)])
                    S.add("pool", cp_fn(ubuf[:, cc, 0:HALO], ubuf[:, cc, TT:TT + HALO]), reads=[("u", cc)], writes=[("u", cc)])
                    S.add("dve", stt_fn(ubuf[:, cc, HALO:HALO + TT], tmp[:, tb, :], 1.0, tmp[:, ta, :], ALU.add, ALU.mult),
                          reads=[("tmp", ta), ("tmp", tb), ("u", cc)], writes=[("u", cc)])
            return None

        def dg_load(cc):
            S.add("pool", (lambda cc: lambda e: e.dma_start(out=dg.rearrange("p k j -> p (k j)"), in_=dgs[cc]))(cc),
                  reads=[("dgs", cc)], writes=[("dg", k) for k in range(NPE)], dma="dgld")

        def pe_evac(cc):
            b = bank()
            S.add("pe", mm_group([(dg[:, k, :], ubuf[:, cc, k:k + TT]) for k in range(NPE)], ps[:, b, :]),
                  reads=[("u", cc)] + [("dg", k) for k in range(NPE)], writes=[("ps", b)])
            S.add("act", act_fn(acc[:, cc, :], ps[:, b, :], AF.Identity, bias=cvc("conv_b", cc)),
                  reads=[("ps", b)] + CONSTS, writes=[("acc", cc)])
            if cc + 1 < 8:
                dg_load(cc + 1)

        def conv_slots():
            def dve_tap(cc, j):
                return lambda: S.add(
                    "dve", stt_fn(acc[:, cc, :], ubuf[:, cc, j:j + TT], cvc("Z", cc * CW + j), acc[:, cc, :], ALU.mult, ALU.add),
                    reads=[("u", cc), ("acc", cc)] + CONSTS, writes=[("acc", cc)])
            slots = []
            for p in range(1, 5):
                prev = (2 * p - 2, 2 * p - 1)
                taps = [dve_tap(cc, j) for j in range(NPE, CW) for cc in prev]
                half = len(taps) // 2
                if p < 4:
                    slots.append((lambda c_: lambda: pe_evac(c_))(2 * p))
                slots += taps[:half]
                if p < 4:
                    slots.append((lambda c_: lambda: pe_evac(c_))(2 * p + 1))
                slots += taps[half:]
            return slots

        def conv_tail(t):
            layer_norm(acc, "acc", ybfB, ysqB, "yB", "cln_g", "cln_b", None, None, usbf, "usbf", silu=True)

        def convo_items(t):
            cslots = {}

            def item(oc):
                def f():
                    if oc % 4 == 0:
                        cslots[oc // 4] = load_piece(t, PI_CONVO + oc // 4)
                    sc_ = cslots[oc // 4]
                    wc_ = wv(sc_, (oc % 4) * 1024, 8, P)
                    bc_ = bank()
                    S.add("pe", mm_group([(wc_[:, kc, :], usbf[:, kc, :]) for kc in range(8)], ps[:, bc_, :]),
                          reads=[("wslot", sc_)] + [("usbf", kc) for kc in range(8)], writes=[("ps", bc_)])
                    S.add("act", act_fn(acc[:, oc, :], ps[:, bc_, :], AF.Copy), reads=[("ps", bc_)], writes=[("acc", oc)])
                return f
            return [item(oc) for oc in range(8)]

        def merge_out(t, buf):
            R = res[buf]
            rk = ("res", buf)
            for oc in range(8):
                sm = load_piece(t, PI_MERGE + oc)
                wgr_ = wv(sm, 0, 8, P)
                wgc_ = wv(sm, 1024, 8, P)
                wro_ = wv(sm, 2048, 16, P)
                b1, b2, b3 = bank(), bank(), bank()
                S.add("pe", mm_group([(wgr_[:, kc, :], xbf[:, kc, :]) for kc in range(8)], ps[:, b1, :]),
                      reads=[("wslot", sm)] + [("xbf", kc) for kc in range(8)], writes=[("ps", b1)])
                S.add("pe", mm_group([(wgc_[:, kc, :], xbf[:, kc, :]) for kc in range(8)], ps[:, b2, :]),
                      reads=[("wslot", sm)] + [("xbf", kc) for kc in range(8)], writes=[("ps", b2)])
                S.add("pe", mm_group([(wro_[:, ec, :], sgate[:, ec, :]) for ec in range(16)], ps[:, b3, :]),
                      reads=[("wslot", sm)] + [("sgate", ec) for ec in range(16)], writes=[("ps", b3)])
                S.add("act", act_fn(tmp[:, 0, :], ps[:, b1, :], AF.Tanh, bias=cvc("Y", 8 + oc), scale=0.5),
                      reads=[("ps", b1)] + CONSTS, writes=[("tmp", 0)])
                S.add("act", act_fn(tmp[:, 1, :], ps[:, b2, :], AF.Tanh, bias=cvc("Y", 16 + oc), scale=0.5),
                      reads=[("ps", b2)] + CONSTS, writes=[("tmp", 1)])
                S.add("dve", stt_fn(tmp[:, 2, :], tmp[:, 0, :], 1.0, ps[:, b3, :], ALU.add, ALU.mult),
                      reads=[("tmp", 0), ("ps", b3)], writes=[("tmp", 2)])
                S.add("dve", stt_fn(tmp[:, 3, :], tmp[:, 1, :], 1.0, acc[:, oc, :], ALU.add, ALU.mult),
                      reads=[("tmp", 1), ("acc", oc)], writes=[("tmp", 3)])
                S.add("pool", tt_fn(merged[:, oc, :], tmp[:, 2, :], tmp[:, 3, :], ALU.add),
                      reads=[("tmp", 2), ("tmp", 3)], writes=[("merged", oc)])
            oslots = {}
            pre = stats_begin()
            for oc in range(8):
                if oc % 4 == 0:
                    oslots[oc // 4] = load_piece(t, PI_OUT + oc // 4)
                so = oslots[oc // 4]
                wo_ = wv(so, (oc % 4) * 1024, 8, P)
                b = bank()
                S.add("pe", mm_group([(wo_[:, kc, :], merged[:, kc, :]) for kc in range(8)], ps[:, b, :]),
                      reads=[("wslot", so)] + [("merged", kc) for kc in range(8)], writes=[("ps", b)])
                S.add("dve", stt_fn(R[:, oc, :], ps[:, b, :], 0.5, R[:, oc, :], ALU.mult, ALU.add),
                      reads=[(rk, oc), ("ps", b)], writes=[(rk, oc)])
                stats_prep(R, rk, ybfA, ysqA, "yA", oc)
                if oc > 0:
                    stats_mm(pre, ybfA, ysqA, "yA", oc - 1)
            stats_mm(pre, ybfA, ysqA, "yA", 7)
            return pre

        def dump(src, keys):
            S.add("act", lambda e: e.dma_start(out=dbg_out, in_=src), reads=keys, dma="dbg")

        def dump16(src, keys, n):
            S.add("act", lambda e: e.dma_start(out=dbg16[:, 0:n, :], in_=src), reads=keys, dma="dbg")

        def load_x(t):
            buf = t % 2
            S.add("sp", (lambda t, buf: lambda e: e.dma_start(out=res[buf][:], in_=xT[t]))(t, buf),
                  writes=[(("res", buf), kc) for kc in range(8)], dma=("xin", buf))

        GU_EARLY = 2
        def build_diags():
            for cc in range(8):
                for k in range(NPE):
                    S.add("dve", ts_fn(dg[:, k, :], ident[:], cvc("Z", cc * CW + k), None, ALU.mult),
                          reads=["ident"] + CONSTS, writes=[("dg", k)])
                S.add("pool", (lambda cc: lambda e: e.dma_start(out=dgs[cc], in_=dg.rearrange("p k j -> p (k j)")))(cc),
                      reads=[("dg", k) for k in range(NPE)], writes=[("dgs", cc)], dma="dgst")

        def xcast(t):
            b_ = t % 2
            for kc in range(8):
                S.add("dve", cp_fn(xbf[:, kc, :], res[b_][:, kc, :]), reads=[(("res", b_), kc)], writes=[("xbf", kc)])

        load_x(0)
        for t in range(ntiles):
            buf = t % 2
            R = res[buf]
            rk = ("res", buf)
            S.add("sp", (lambda t: lambda e: e.dma_start(out=cst[:], in_=csd[t]))(t), writes=["cst"], dma="cst")
            if t == 0:
                xcast(0)
            def finish():
                S.add("act", (lambda t, buf: lambda e: e.dma_start(out=outT[t], in_=res[buf][:]))(t, buf),
                      reads=[(rk, kc) for kc in range(8)], dma=("out", buf))
            if stop == "pro":
                finish(); continue
            ffn_gu(t, PI_FFN1, 0 if t == 0 else GU_EARLY, NFC // 2)
            if t == 0:
                build_diags()
            pre1 = ffn_down(t, buf, PI_FFN1, first=True)
            layer_norm(R, rk, ybfA, ysqA, "yA", "ln1_g", "ln1_b", "A_g1", "A_b1", xbf, "xbf", pre=pre1)
            if stop == "ln1":
                finish(); continue
            if dbg == "ln1" and t == ntiles - 1:
                dump(R[:], [(rk, kc) for kc in range(8)])
            glu_conv(t)
            cops = conv_slots()
            cpos = 0
            if stop == "glu":
                for c_ in cops:
                    c_()
                finish(); continue
            items = proj_items(t, 0)
            if stop is not None and stop.startswith("p_"):
                for it in items[:int(stop[2:])]:
                    it()
                finish(); continue
            for it in items:
                it()

            def fill(n):
                nonlocal cpos
                for _ in range(n):
                    if cpos < len(cops):
                        cops[cpos]()
                        cpos += 1

            conv_done = False
            import os as _os
            for h in range(int(_os.environ.get("KHEADS", H))):
                nxt = proj_items(t, h + 1) if h + 1 < H else None
                if nxt is None:
                    while cpos < len(cops):
                        fill(8)
                    if not conv_done:
                        conv_tail(t)
                        conv_done = True
                    nxt = convo_items(t)
                if _os.environ.get("KNOFILL"):
                    retention(t, h, nxt, lambda n: None)
                else:
                    retention(t, h, nxt, fill)
                if cpos >= len(cops) and not conv_done:
                    conv_tail(t)
                    conv_done = True
            while cpos < len(cops):
                fill(8)
            if not conv_done:
                conv_tail(t)
            if stop == "heads":
                finish(); continue
            if dbg == "rg" and t == ntiles - 1:
                dump16(sgate, [("sgate", ec) for ec in range(16)], 16)
            if dbg == "us" and t == ntiles - 1:
                dump16(usbf, [("usbf", kc) for kc in range(8)], 8)
            if stop == "convln":
                finish(); continue
            if t + 1 < ntiles:
                load_x(t + 1)
            pre2 = merge_out(t, buf)
            if dbg == "merged" and t == ntiles - 1:
                dump16(merged, [("merged", kc) for kc in range(8)], 8)
            if dbg == "y2" and t == ntiles - 1:
                dump(R[:], [(rk, kc) for kc in range(8)])
            if stop == "merge":
                finish(); continue
            layer_norm(R, rk, ybfA, ysqA, "yA", "ln2_g", "ln2_b", "A_g2", "A_b2", xbf, "xbf", pre=pre2)
            ffn_gu(t, PI_FFN2, 0, NFC // 2)
            if t + 1 < ntiles:
                xcast(t + 1)
            pre3 = ffn_down(t, buf, PI_FFN2, first=False)
            if t + 1 < ntiles:
                ffn_gu(t + 1, PI_FFN1, 0, GU_EARLY)
            layer_norm(R, rk, ybfA, ysqA, "yA", None, None, "ln3_g", "ln3_b", None, None, pre=pre3)
            S.add("act", (lambda t, buf: lambda e: e.dma_start(out=outT[t], in_=res[buf][:]))(t, buf),
                  reads=[(rk, kc) for kc in range(8)], dma=("out", buf))

        S.finalize()
        sems = {}
        for k in S.sem_keys():
            nm = "s_" + "".join(ch if ch.isalnum() else "_" for ch in str(k))
            sems[k] = es.enter_context(nc.semaphore(nm))
        with nc.Block() as block:
            @block.tensor
            def _(e):
                S.run_engine("pe", e, sems)

            @block.scalar
            def _(e):
                S.run_engine("act", e, sems)
                for k, c in S.dma_count.items():
                    if isinstance(k, tuple) and k[0] == "out" or k == "dbg":
                        e.wait_ge(sems[("dma", k)], 16 * c)

            @block.vector
            def _(e):
                S.run_engine("dve", e, sems)

            @block.gpsimd
            def _(e):
                S.run_engine("pool", e, sems)

            @block.sync
            def _(e):
                S.run_engine("sp", e, sems)
    return nc


def _prep_inputs(inputs):
    inp = {k: np.asarray(v) for k, v in inputs.items()}
    wall = _build_wall(inp)
    cv, bv, cs, dm, xit, _ = _build_consts(inp)
    x = np.asarray(inp["x"], np.float32)
    in_maps = []
    for b in range(x.shape[0]):
        xT = np.ascontiguousarray(x[b].reshape(NT, TT, 8, P).transpose(0, 3, 2, 1))
        in_maps.append({"xT": xT, "wall": wall, "cvec": cv, "bvrow": bv, "cs": cs, "dmask": dm, "xit": xit})
    return in_maps


def kernel(**inputs):
    in_maps = _prep_inputs(inputs)
    nc = build_program(NT)
    res = run_bass_kernel_spmd(nc, in_maps, core_ids=list(range(len(in_maps))))
    outs = []
    for r in res.results:
        oT = np.asarray(r["outT"], np.float32)
        outs.append(oT.transpose(0, 3, 2, 1).reshape(SEQ, D))
    return np.stack(outs, 0).astype(np.float32)
```
